# Optimizing a Trainium2 kernel written in Bass

```python
import math
import jax, jax.numpy as jnp
from jax import lax
import numpy as np

D_MODEL = 1024
BATCH = 32
SEQ = 256
DEPTH = 1
DEC_BATCH = 4
DEC_SEQ = 2048
PAST_LEN = 512

GRID_W = 64
N_HEADS = 8
HEAD_DIM = 64
D_ATT = N_HEADS * HEAD_DIM
N_POOL_GROUPS = 4
POOL_WINDOWS = (2, 4, 8, 16)
D_POOL = 512
POOL_GROUP_DIM = D_POOL // N_POOL_GROUPS
N_BRANCHES = 2
D_IN = D_POOL + 3 * D_ATT + N_BRANCHES * D_MODEL
D_FF = 2816
NA_KH = 8
NA_KW = 16
Q_BLOCK = 128
N_MOD = 9
EPS = 1e-6
NEG = -1e30

kernel_name = "hybrid_pool_natten_macaron_step"


def rms_norm(x, g):
    xf = x.astype(jnp.float32)
    y = xf * lax.rsqrt(jnp.mean(xf * xf, axis=-1, keepdims=True) + EPS)
    return (y * g.astype(jnp.float32)).astype(x.dtype)


def adaln_params(cond, w_ada, b_ada):
    m = jax.nn.silu(cond) @ w_ada + b_ada
    m = m.reshape(-1, 1, N_MOD, D_MODEL)
    return [m[:, :, i] for i in range(N_MOD)]


def modulate(h, shift, scale):
    return h * (1.0 + scale) + shift


def swiglu(h, w_gu, w_dn):
    a, u = jnp.split(h @ w_gu, 2, axis=-1)
    return (jax.nn.silu(a) * u) @ w_dn


def pool_mixer(p, w_pool, pool_scale):
    B, L, _ = p.shape
    pg = p.reshape(B, L, N_POOL_GROUPS, POOL_GROUP_DIM)
    cs = jnp.cumsum(pg.astype(jnp.float32), axis=1)
    cs = jnp.concatenate([jnp.zeros_like(cs[:, :1]), cs], axis=1)
    t = jnp.arange(L)
    means = []
    for g, w in enumerate(POOL_WINDOWS):
        lo = jnp.clip(t - w // 2, 0, L)
        hi = jnp.clip(t + w // 2, 0, L)
        s = cs[:, hi, g] - cs[:, lo, g]
        means.append(s / (hi - lo).astype(jnp.float32)[None, :, None])
    mean = jnp.stack(means, axis=2)
    d = (mean - pg.astype(jnp.float32)).astype(p.dtype)
    y = jnp.einsum('blgc,gcd->blgd', d, w_pool)
    return y.reshape(B, L, D_POOL) * pool_scale


def context_attention(q, k, v):
    B, P, H, d = q.shape
    scale = 1.0 / math.sqrt(HEAD_DIM)
    qb = q.reshape(B, P // Q_BLOCK, Q_BLOCK, H, d).transpose(1, 0, 2, 3, 4)

    def one_block(q_blk):
        s = jnp.einsum('bqhd,bkhd->bhqk', q_blk, k).astype(jnp.float32) * scale
        pr = jax.nn.softmax(s, axis=-1).astype(v.dtype)
        return jnp.einsum('bhqk,bkhd->bqhd', pr, v)

    o = lax.map(one_block, qb)
    return o.transpose(1, 0, 2, 3, 4).reshape(B, P, H * d)


def latent_attention(q, k, v, k_ctx, v_ctx, rpb):
    B, L, H, d = q.shape
    rows = L // GRID_W
    kh = min(NA_KH, rows)
    scale = 1.0 / math.sqrt(HEAD_DIM)
    qg = q.reshape(B, rows, GRID_W, H, d)
    kg = k.reshape(B, rows, GRID_W, H, d)
    vg = v.reshape(B, rows, GRID_W, H, d)
    r = jnp.arange(rows)
    r0 = jnp.clip(r - kh // 2, 0, rows - kh)
    col = jnp.arange(GRID_W)
    c0 = jnp.clip(col - NA_KW // 2, 0, GRID_W - NA_KW)
    col_ok = (col[None, :] >= c0[:, None]) & (col[None, :] < c0[:, None] + NA_KW)
    dc_idx = jnp.clip(col[None, :] - col[:, None] + NA_KW - 1, 0, 2 * NA_KW - 2)
    col_bias = rpb.astype(jnp.float32)[:, :, dc_idx]
    col_bias = jnp.where(col_ok[None, None], col_bias, NEG)

    def one_row(args):
        q_row, r_q, r_start = args
        k_win = lax.dynamic_slice_in_dim(kg, r_start, kh, axis=1)
        v_win = lax.dynamic_slice_in_dim(vg, r_start, kh, axis=1)
        dr_idx = r_start + jnp.arange(kh) - r_q + NA_KH - 1
        bias = col_bias[:, dr_idx].transpose(0, 2, 1, 3)
        s_loc = jnp.einsum('bqhd,bikhd->bhqik', q_row, k_win).astype(jnp.float32) * scale
        s_loc = (s_loc + bias[None]).reshape(B, H, GRID_W, kh * GRID_W)
        s_ctx = jnp.einsum('bqhd,bphd->bhqp', q_row, k_ctx).astype(jnp.float32) * scale
        pr = jax.nn.softmax(jnp.concatenate([s_loc, s_ctx], axis=-1), axis=-1).astype(v.dtype)
        p_loc = pr[..., :kh * GRID_W].reshape(B, H, GRID_W, kh, GRID_W)
        p_ctx = pr[..., kh * GRID_W:]
        return (jnp.einsum('bhqik,bikhd->bqhd', p_loc, v_win)
                + jnp.einsum('bhqp,bphd->bqhd', p_ctx, v_ctx))

    o = lax.map(one_row, (qg.transpose(1, 0, 2, 3, 4), r, r0))
    return o.transpose(1, 0, 2, 3, 4).reshape(B, L, H * d)


def trunk_layer(x, mod, lp, k_ctx=None, v_ctx=None):
    sh1, sc1, g1, sh2, sc2, g2, sh3, sc3, g3 = mod
    B, L, _ = x.shape
    h = modulate(rms_norm(x, lp['g_ff1']), sh1, sc1)
    x = x + 0.5 * g1 * swiglu(h, lp['w_ff1_in'], lp['w_ff1_out'])

    h = modulate(rms_norm(x, lp['g_mix']), sh2, sc2)
    proj = h @ lp['w_in']
    o1 = D_POOL
    o2 = o1 + D_ATT
    o3 = o2 + D_ATT
    o4 = o3 + D_ATT
    p = proj[..., :o1]
    q = rms_norm(proj[..., o1:o2].reshape(B, L, N_HEADS, HEAD_DIM), lp['q_gain'])
    k = rms_norm(proj[..., o2:o3].reshape(B, L, N_HEADS, HEAD_DIM), lp['k_gain'])
    v = proj[..., o3:o4].reshape(B, L, N_HEADS, HEAD_DIM)
    gates = jax.nn.sigmoid(proj[..., o4:].astype(jnp.float32)).astype(x.dtype)
    gates = gates.reshape(B, L, N_BRANCHES, D_MODEL)

    a = pool_mixer(p, lp['w_pool'], lp['pool_scale']) @ lp['w_br_pool']
    if k_ctx is None:
        att = context_attention(q, k, v)
    else:
        att = latent_attention(q, k, v, k_ctx, v_ctx, lp['rpb'])
    b = att @ lp['w_br_att']
    merged = gates[:, :, 0] * a + gates[:, :, 1] * b
    x = x + g2 * (merged @ lp['w_out'])

    h = modulate(rms_norm(x, lp['g_ff2']), sh3, sc3)
    x = x + 0.5 * g3 * swiglu(h, lp['w_ff2_in'], lp['w_ff2_out'])
    return x, k, v


def setup_inputs(seed: int = 0) -> dict:
    key = jax.random.key(seed)
    ks = jax.random.split(key, 32)
    f32 = jnp.float32

    def nrm(k, shape, scale=1.0):
        return jax.random.normal(k, shape, f32) * scale

    def gain(k, shape):
        return 1.0 + 0.1 * jax.random.normal(k, shape, f32)

    D = D_MODEL
    return {
        "x_prompt": nrm(ks[0], (BATCH, SEQ, D)),
        "x_sample": nrm(ks[1], (DEC_BATCH, DEC_SEQ, D)),
        "cache_k": nrm(ks[2], (DEC_BATCH, DEPTH, PAST_LEN, N_HEADS, HEAD_DIM)),
        "cache_v": nrm(ks[3], (DEC_BATCH, DEPTH, PAST_LEN, N_HEADS, HEAD_DIM)),
        "c": nrm(ks[4], (DEC_BATCH, D)),
        "c_ctx": nrm(ks[5], (D,)),
        "w_ada": nrm(ks[6], (DEPTH, D, N_MOD * D), D ** -0.5),
        "b_ada": nrm(ks[7], (DEPTH, N_MOD * D), 0.02),
        "g_ff1": gain(ks[8], (DEPTH, D)),
        "w_ff1_in": nrm(ks[9], (DEPTH, D, 2 * D_FF), D ** -0.5),
        "w_ff1_out": nrm(ks[10], (DEPTH, D_FF, D), D_FF ** -0.5),
        "g_mix": gain(ks[11], (DEPTH, D)),
        "w_in": nrm(ks[12], (DEPTH, D, D_IN), D ** -0.5),
        "q_gain": gain(ks[13], (DEPTH, HEAD_DIM)),
        "k_gain": gain(ks[14], (DEPTH, HEAD_DIM)),
        "w_pool": nrm(ks[15], (DEPTH, N_POOL_GROUPS, POOL_GROUP_DIM, POOL_GROUP_DIM), POOL_GROUP_DIM ** -0.5),
        "pool_scale": gain(ks[16], (DEPTH, D_POOL)),
        "rpb": nrm(ks[17], (DEPTH, N_HEADS, 2 * NA_KH - 1, 2 * NA_KW - 1), 0.1),
        "w_br_pool": nrm(ks[18], (DEPTH, D_POOL, D), D_POOL ** -0.5),
        "w_br_att": nrm(ks[19], (DEPTH, D_ATT, D), D_ATT ** -0.5),
        "w_out": nrm(ks[20], (DEPTH, D, D), D ** -0.5),
        "g_ff2": gain(ks[21], (DEPTH, D)),
        "w_ff2_in": nrm(ks[22], (DEPTH, D, 2 * D_FF), D ** -0.5),
        "w_ff2_out": nrm(ks[23], (DEPTH, D_FF, D), D_FF ** -0.5),
    }


def reference(x_prompt, x_sample, cache_k, cache_v, c, c_ctx, w_ada, b_ada, g_ff1, w_ff1_in,
              w_ff1_out, g_mix, w_in, q_gain, k_gain, w_pool, pool_scale, rpb, w_br_pool,
              w_br_att, w_out, g_ff2, w_ff2_in, w_ff2_out):
    xp = x_prompt
    xs = x_sample
    new_ks = []
    new_vs = []
    for l in range(DEPTH):
        lp = {
            'g_ff1': g_ff1[l], 'w_ff1_in': w_ff1_in[l], 'w_ff1_out': w_ff1_out[l],
            'g_mix': g_mix[l], 'w_in': w_in[l], 'q_gain': q_gain[l], 'k_gain': k_gain[l],
            'w_pool': w_pool[l], 'pool_scale': pool_scale[l], 'rpb': rpb[l],
            'w_br_pool': w_br_pool[l], 'w_br_att': w_br_att[l], 'w_out': w_out[l],
            'g_ff2': g_ff2[l], 'w_ff2_in': w_ff2_in[l], 'w_ff2_out': w_ff2_out[l],
        }
        mod_ctx = adaln_params(c_ctx, w_ada[l], b_ada[l])
        xp, k_p, v_p = trunk_layer(xp, mod_ctx, lp)
        new_ks.append(k_p)
        new_vs.append(v_p)
        mod_lat = adaln_params(c, w_ada[l], b_ada[l])
        xs, _, _ = trunk_layer(xs, mod_lat, lp, cache_k[:, l], cache_v[:, l])
    new_k = jnp.stack(new_ks, axis=1)
    new_v = jnp.stack(new_vs, axis=1)
    return (xp, xs, new_k, new_v)
```

```python
import contextlib
import numpy as np
import concourse.bass as bass
import concourse.mybir as mybir
from concourse.bass_utils import run_bass_kernel_spmd

F32 = mybir.dt.float32
BF16 = mybir.dt.bfloat16
AF = mybir.ActivationFunctionType
ALU = mybir.AluOpType

D = 1024
DFF = 2816
NJ = DFF // 128
NCORES = 8
NTOK_P = 1024
NTOK_S = 1024
NHALO = 256
NOWN = NTOK_P + NTOK_S
NALL = NOWN + NHALO
EPS = 1e-6
NVM = 42
NCST = 2 + 64 + 64 + 2 * NVM
NSLOT = 8
SLOT_ELEMS = 2048

TT_ALL = [(0, 512, 0), (512, 512, 0), (1024, 512, 1), (1536, 512, 1), (2048, 256, 1)]
TT_OWN = TT_ALL[:4]


def qb_tiles():
    tl = []
    for b in range(8):
        if b <= 1:
            cs = [0, 1, 2, 3]
        elif b >= 6:
            cs = [4, 5, 6, 7]
        else:
            cs = list(range(b - 2, b + 3))
        ent = [("o", c, c - b) for c in cs]
        if b == 0:
            ent += [("h", 0, -2), ("h", 1, -1)]
        if b == 1:
            ent += [("h", 1, -2)]
        if b == 6:
            ent += [("h", 0, 2)]
        if b == 7:
            ent += [("h", 0, 1), ("h", 1, 2)]
        tl.append(ent)
    return tl


class Tok:
    __slots__ = ("sem", "val")

    def __init__(self, sem, val):
        self.sem = sem
        self.val = val


class Res:
    __slots__ = ("writer", "readers")

    def __init__(self):
        self.writer = None
        self.readers = []


class Prog:
    ENGS = ("pe", "act", "dve", "pool", "sp")

    def __init__(self, nc, es):
        self.nc = nc
        self.es = es
        self.sems = {}
        self.cnt = {}
        self.streams = {e: [] for e in self.ENGS}
        self.seen = {e: {} for e in self.ENGS}
        self.res = {}
        self.pending = {}
        for e in ("pe", "act", "dve", "pool"):
            self.sem(e)

    def barrier(self):
        toks = [Tok(e, self.cnt[e]) for e in ("pe", "act", "dve") if self.cnt[e] > 0]
        toks += [Tok(s, v) for s, v in self.cnt.items()
                 if s not in ("pe", "act", "dve", "pool") and not s.startswith("w") and v > 0]
        for e in ("pe", "act", "dve", "sp"):
            self.pending.setdefault(e, []).extend(toks)

    def sem(self, name):
        if name not in self.sems:
            self.sems[name] = self.es.enter_context(self.nc.semaphore("s_" + name))
            self.cnt[name] = 0
        return self.sems[name]

    def R(self, *key):
        r = self.res.get(key)
        if r is None:
            r = Res()
            self.res[key] = r
        return r

    def emit(self, eng, fn, reads=(), writes=(), deps=(), inc=True, dma_sem=None):
        toks = [t for t in deps if t is not None]
        toks.extend(self.pending.pop(eng, []))
        for r in reads:
            if r.writer is not None:
                toks.append(r.writer)
        for w in writes:
            if w.writer is not None:
                toks.append(w.writer)
            toks.extend(w.readers)
        need = {}
        for t in toks:
            if eng == "pe" and t.sem == "pe":
                continue
            if need.get(t.sem, 0) < t.val:
                need[t.sem] = t.val
        waits = []
        seen = self.seen[eng]
        for s, v in need.items():
            if seen.get(s, 0) >= v:
                continue
            seen[s] = v
            waits.append((s, v))
        if dma_sem is not None:
            self.sem(dma_sem)
            self.cnt[dma_sem] += 16
            tok = Tok(dma_sem, self.cnt[dma_sem])
            incspec = (dma_sem, 16)
        elif inc:
            self.cnt[eng] += 1
            tok = Tok(eng, self.cnt[eng])
            incspec = (eng, 1)
        else:
            tok = Tok(eng, self.cnt[eng] + 1)
            incspec = None
        self.streams[eng].append((waits, fn, incspec))
        for r in reads:
            r.readers.append(tok)
        for w in writes:
            w.writer = tok
            w.readers = []
        return tok

    def replay(self, eng, engine, final_waits=()):
        for waits, fn, incspec in self.streams[eng]:
            for s, v in waits:
                engine.wait_ge(self.sems[s], v)
            ins = fn(engine)
            if incspec is not None:
                ins.then_inc(self.sems[incspec[0]], incspec[1])
        for s, v in final_waits:
            engine.wait_ge(self.sems[s], v)


class Arena:
    def __init__(self, ap, nwords):
        self.ap = ap
        self.n = nwords
        self.off = 0

    def mark(self):
        return self.off

    def reset(self, m):
        self.off = m

    def take(self, nbytes, dtype, pattern=None, **dims):
        nw = (nbytes + 3) // 4
        nw = (nw + 7) // 8 * 8
        assert self.off + nw <= self.n, ("SBUF arena overflow", self.off, nw, self.n)
        v = self.ap[:, self.off:self.off + nw]
        self.off += nw
        if dtype == BF16:
            v = v.bitcast(BF16)[:, 0:nbytes // 2]
        else:
            v = v[:, 0:nbytes // 4]
        if pattern is not None:
            v = v.rearrange(pattern, **dims)
        return v


class PsumPool:
    def __init__(self, P, banks, name):
        self.P = P
        self.banks = banks
        self.i = 0
        self.name = name

    def next(self):
        b = self.banks[self.i % len(self.banks)]
        self.i += 1
        return b


def build_program(with_mixer=True, debug=False, stops=("merge", "merge")):
    LV = {None: 0, "q": 0.3, "k": 0.6, "qkv": 1, "att": 2, "pool": 3, "merge": 4}
    nc = bass.Bass("TRN2", target_bir_lowering=False)
    es = contextlib.ExitStack()

    def din(name, shape):
        return nc.dram_tensor(name, list(shape), F32, kind="ExternalInput").ap()

    def dout(name, shape):
        return nc.dram_tensor(name, list(shape), F32, kind="ExternalOutput").ap()

    xin = din("xin", [NALL, D])
    condT = din("condT", [128, 16])
    w_ada = din("w_ada", [D, 9 * D])
    b_adaT = din("b_adaT", [128, 72])
    gnT = din("gnT", [128, 24])
    w_ff_in = [din("w_ff1_in", [D, 2 * DFF]), din("w_ff2_in", [D, 2 * DFF])]
    w_ff_out = [din("w_ff1_out", [DFF, D]), din("w_ff2_out", [DFF, D])]
    ident_d = din("ident", [128, 128])
    w_in = din("w_in", [D, 4096])
    w_pool = din("w_pool", [4, 128, 128])
    w_br_pool = din("w_br_pool", [512, D])
    w_br_att = din("w_br_att", [512, D])
    w_out = din("w_out", [D, D])
    gainT = din("gainT", [128, 6])
    ck_d = din("ck", [512, 512])
    cv_d = din("cv", [512, 512])
    eb2_d = din("eb2src", [128, 8 * 14 * 64])
    cmask_d = din("colmask", [128, 64])
    cst_d = din("cst", [128, NCST])
    yo = dout("yo", [NOWN, D])
    nk_d = dout("nk", [NTOK_P, 512])
    nv_d = dout("nv", [NTOK_P, 512])

    with es:
        arena_t = es.enter_context(nc.sbuf_tensor("arena", [128, 51200], F32))
        psum_t = [es.enter_context(nc.psum_tensor("ps%d" % i, [128, 512], F32)) for i in range(8)]
        P = Prog(nc, es)
        A = Arena(arena_t[:, :], 51200)

        X = A.take(8 * NOWN * 4, F32, "p (c t) -> p c t", c=8)
        XN = A.take(8 * NALL * 2, BF16, "p (c t) -> p c t", c=8)
        WR = A.take(NSLOT * SLOT_ELEMS * 2, BF16, "p (s e) -> p s e", s=NSLOT)
        IDENT = A.take(128 * 4, F32)
        ONES = A.take(128 * 2, BF16)
        MODT = A.take(72 * 2 * 4, F32, "p (b k) -> p b k", k=2)
        BADA = A.take(72 * 4, F32)
        GN = A.take(24 * 4, F32, "p (n c) -> p n c", n=3)
        DER = A.take(9 * 8 * 2 * 4, F32, "p (i c k) -> p i c k", i=9, c=8)
        CONDF = A.take(16 * 4, F32)
        CONDB = A.take(16 * 2, BF16, "p (c k) -> p c k", k=2)
        scratch0 = A.mark()

        banks = [P.R("bank", i) for i in range(8)]
        poolA = PsumPool(P, [0, 1, 2, 3], "A")
        poolB = PsumPool(P, [4, 5], "B")
        poolC = PsumPool(P, [6, 7], "C")

        def bank_ap(b):
            return psum_t[b]

        wsched = []

        def plan(tag, kind, ap, a0, n):
            wsched.append((tag, kind, ap, a0, n))

        wstate = {"next_load": 0, "next_acq": 0, "released": set()}
        wslot_res = [P.R("wslot", s) for s in range(NSLOT)]

        def _wload(u):
            tag, kind, ap, a0, n = wsched[u]
            s = u % NSLOT
            if kind == "kxn":
                nkc = ap.shape[0] // 128
                src = ap[:, a0:a0 + n].rearrange("(kc p) n -> p kc n", p=128)
                dst = WR[:, s, 0:nkc * n].rearrange("p (kc n) -> p kc n", kc=nkc)
            else:
                src = ap[a0:a0 + n * 128, :].rearrange("(r p) n -> p r n", p=128)
                dst = WR[:, s, 0:n * 1024].rearrange("p (r n) -> p r n", r=n)
            P.emit("pool", lambda e, dst=dst, src=src: e.dma_start(out=dst, in_=src),
                   writes=[wslot_res[s]], dma_sem="w%d" % s)

        def _wpump():
            while wstate["next_load"] < len(wsched):
                u = wstate["next_load"]
                if u >= NSLOT and (u - NSLOT) not in wstate["released"]:
                    break
                _wload(u)
                wstate["next_load"] += 1

        def wacq(tag):
            u = wstate["next_acq"]
            wstate["next_acq"] += 1
            assert wsched[u][0] == tag, (wsched[u][0], tag)
            _wpump()
            assert wstate["next_load"] > u, "weight ring deadlock"
            _, kind, ap, a0, n = wsched[u]
            s = u % NSLOT
            if kind == "kxn":
                nkc = ap.shape[0] // 128
                view = WR[:, s, 0:nkc * n].rearrange("p (kc n) -> p kc n", kc=nkc)
            else:
                view = WR[:, s, 0:n * 1024].rearrange("p (r n) -> p r n", r=n)
            return u, view, wslot_res[s]

        def wrel(u):
            wstate["released"].add(u)
            _wpump()

        ff_blocks = [(j0, 2) for j0 in range(0, NJ, 2)]

        def plan_ada(u):
            plan(("ada", u), "kxn", w_ada, u * 256, 256)

        def plan_ff(which, bi):
            j0, nj = ff_blocks[bi]
            plan(("ffg", which, bi), "kxn", w_ff_in[which], j0 * 128, nj * 128)
            plan(("ffu", which, bi), "kxn", w_ff_in[which], DFF + j0 * 128, nj * 128)
            plan(("ffd", which, bi), "rows", w_ff_out[which], j0 * 128, nj)

        for u in range(8):
            plan_ada(u)
        ada_next = 12
        for bi in range(len(ff_blocks)):
            plan_ff(0, bi)
            if bi == 0:
                for u in range(8, 12):
                    plan_ada(u)
            for _ in range(3):
                if ada_next < 36:
                    plan_ada(ada_next)
                    ada_next += 1
        assert ada_next == 36
        if with_mixer:
            for half in range(2):
                lv = LV[stops[half]]
                for nm, c0 in (("q", 512), ("k", 1024), ("v", 1536), ("p", 0)):
                    if lv < {"q": 0.3, "k": 0.6, "v": 1, "p": 3}[nm]:
                        continue
                    for u in range(2):
                        plan(("win", half, nm, u), "kxn", w_in, c0 + u * 256, 256)
                if lv < 4:
                    continue
                for jp in range(4):
                    plan(("g0", half, jp), "kxn", w_in, 2048 + jp * 256, 256)
                    plan(("g1", half, jp), "kxn", w_in, 3072 + jp * 256, 256)
                    plan(("brp", half, jp), "kxn", w_br_pool, jp * 256, 256)
                    plan(("bra", half, jp), "kxn", w_br_att, jp * 256, 256)
                for mu in range(4):
                    plan(("wo", half, mu), "kxn", w_out, mu * 256, 256)
        for bi in range(len(ff_blocks)):
            plan_ff(1, bi)

        def mm(out, lhsT, rhs, start, stop, reads, writes, inc, sgc=False):
            if sgc:
                return P.emit("pe", lambda e: e.matmul(out, lhsT=lhsT, rhs=rhs, start=start, stop=stop,
                                                       skip_group_check=True),
                              reads=reads, writes=writes, inc=inc)
            return P.emit("pe", lambda e: e.matmul(out, lhsT=lhsT, rhs=rhs, start=start, stop=stop),
                          reads=reads, writes=writes, inc=inc)

        def tr(out, in_, reads, writes, inc):
            return P.emit("pe", lambda e: e.transpose(out=out, in_=in_, identity=IDENT),
                          reads=list(reads) + [r_const], writes=writes, inc=inc)

        def actv(out, in_, func, reads, writes, bias=None, scale=None):
            kw = {}
            if bias is not None:
                kw["bias"] = bias
            if scale is not None:
                kw["scale"] = scale
            return P.emit("act", lambda e: e.activation(out=out, in_=in_, func=func, **kw),
                          reads=reads, writes=writes)

        def vtt(out, in0, in1, op, reads, writes):
            return P.emit("dve", lambda e: e.tensor_tensor(out=out, in0=in0, in1=in1, op=op),
                          reads=reads, writes=writes)

        def vstt(out, in0, scalar, in1, op0, op1, reads, writes):
            return P.emit("dve", lambda e: e.scalar_tensor_tensor(out=out, in0=in0, scalar=scalar, in1=in1,
                                                                   op0=op0, op1=op1),
                          reads=reads, writes=writes)

        def vts(out, in0, s1, op0, reads, writes, s2=None, op1=None):
            if op1 is None:
                return P.emit("dve", lambda e: e.tensor_scalar(out=out, in0=in0, scalar1=s1, scalar2=None, op0=op0),
                              reads=reads, writes=writes)
            return P.emit("dve", lambda e: e.tensor_scalar(out=out, in0=in0, scalar1=s1, scalar2=s2,
                                                            op0=op0, op1=op1), reads=reads, writes=writes)

        def vcopy(out, in_, reads, writes):
            return P.emit("dve", lambda e: e.tensor_copy(out=out, in_=in_), reads=reads, writes=writes)

        def vrecip(out, in_, reads, writes):
            return P.emit("dve", lambda e: e.reciprocal(out=out, in_=in_), reads=reads, writes=writes)

        def vmemset(ap, val, writes):
            return P.emit("dve", lambda e: e.memset(ap, val), writes=writes)

        def sp_dma(out, in_, sem, reads=(), writes=()):
            return P.emit("sp", lambda e: e.dma_start(out=out, in_=in_), reads=reads, writes=writes, dma_sem=sem)

        r_const = P.R("const")
        sp_dma(IDENT, ident_d[:, :], "const", writes=[r_const])
        sp_dma(BADA, b_adaT[:, :], "const", writes=[r_const])
        sp_dma(GN.rearrange("p n c -> p (n c)"), gnT[:, :], "const", writes=[r_const])
        sp_dma(CONDF, condT[:, :], "const", writes=[r_const])
        r_ones = P.R("ones")
        vmemset(ONES, 1.0 / 1024.0, [r_ones])
        EPSC = A.take(4, F32)
        r_eps = P.R("epsc")
        vmemset(EPSC, EPS, [r_eps])
        r_condb = P.R("condb")
        actv(CONDB.rearrange("p c k -> p (c k)"), CONDF, AF.Silu, [r_const], [r_condb])

        WPOOL = A.take(4 * 128 * 2, BF16, "p (g d) -> p g d", g=4)
        ONESBLK = A.take(128 * 2, BF16)
        ONESB = A.take(64 * 2, BF16)
        GAINS = A.take(6 * 4, F32)
        QG = A.take(4, F32)
        CST = A.take(NCST * 4, F32)
        CMASK = A.take(64 * 4, F32)
        r_mc = P.R("mixconst")
        r_wpool = P.R("wpool")
        if with_mixer:
            sp_dma(GAINS, gainT[:, :], "const", writes=[r_const])
            sp_dma(CST, cst_d[:, :], "const", writes=[r_const])
            sp_dma(CMASK, cmask_d[:, :], "const", writes=[r_const])
            P.emit("pool", lambda e: e.dma_start(out=WPOOL, in_=w_pool.rearrange("g c d -> c g d")),
                   writes=[r_wpool], dma_sem="wpool")
            vmemset(ONESBLK, 0.0, [r_mc])
            vmemset(ONESBLK[0:64, 0:64], 1.0 / 64.0, [r_mc])
            vmemset(ONESBLK[64:128, 64:128], 1.0 / 64.0, [r_mc])
            vmemset(ONESB, 1.0, [r_mc])
            vts(QG, GAINS[:, 0:1], 0.125, ALU.mult, [r_const], [r_mc])
        KG = GAINS[:, 1:2]
        PSC = GAINS[:, 2:6]
        FLG = CST[:, 0:2]
        EDGE_P = CST[:, 2:66].rearrange("p (g e) -> p g e", g=4)
        EDGE_S = CST[:, 66:130].rearrange("p (g e) -> p g e", g=4)
        VM = CST[:, 130:130 + 2 * NVM]

        mix0 = A.mark()
        XH = A.take(8 * NHALO * 4, F32, "p (c t) -> p c t", c=8)
        RSTD = A.take(NALL * 4, F32)
        TMPA = A.take(2 * 512 * 4, F32, "p (s t) -> p s t", s=2)
        TMPB = A.take(2 * 512 * 4, F32, "p (s t) -> p s t", s=2)
        HB = A.take(2 * 2 * 512 * 2, BF16, "p (s j t) -> p s j t", s=2, j=2)
        SIL = A.take(2 * 512 * 4, F32, "p (s t) -> p s t", s=2)
        STG = A.take(4 * 1024 * 4, F32, "p (s t) -> p s t", s=4)
        ffend = A.mark()

        def xview(c, t0, n):
            if t0 < NOWN:
                return X[:, c, t0:t0 + n]
            return XH[:, c, t0 - NOWN:t0 - NOWN + n]

        def rx(c, t0):
            return P.R("x", c, t0 // 512)

        def rxn(c, t0):
            return P.R("xn", c, t0 // 512)

        stg_res = [[P.R("stg", s, h) for h in range(2)] for s in range(4)]

        def load_x_tile(tt):
            t00, nn, _c = TT_ALL[tt]
            for i in range(t00 // 128, (t00 + nn) // 128):
                load_x_sub(i)

        def load_x_sub(i):
            s = i % 4
            sp_dma(STG[:, s, :], xin[i * 128:(i + 1) * 128, :], "stg%d" % s, writes=stg_res[s])
            t0 = i * 128
            for half in range(2):
                b = poolC.next()
                for cc in range(4):
                    c = half * 4 + cc
                    tr(bank_ap(b)[:, cc * 128:(cc + 1) * 128], STG[:, s, c * 128:(c + 1) * 128],
                       [stg_res[s][half]], [banks[b]], inc=(cc == 3))
                if t0 < NOWN:
                    dst = X[:, half * 4:half * 4 + 4, t0:t0 + 128]
                else:
                    dst = XH[:, half * 4:half * 4 + 4, t0 - NOWN:t0 - NOWN + 128]
                wr = [rx(half * 4 + cc, t0) for cc in range(4)]
                srcv = bank_ap(b)[:, :].rearrange("p (c t) -> p c t", c=4)
                if half == 0:
                    actv(dst, srcv, AF.Copy, [banks[b]], wr)
                else:
                    vcopy(dst, srcv, [banks[b]], wr)

        load_x_tile(0)
        load_x_tile(1)

        r_mod = P.R("modT")
        r_der = [P.R("der", i) for i in range(9)]

        def ada_units(us):
            b = poolC.next()
            col = 0
            blks = []
            for u in us:
                uu, view, wres = wacq(("ada", u))
                for bb in range(2):
                    blk = 2 * u + bb
                    for kc in range(8):
                        mm(bank_ap(b)[:, col:col + 2], view[:, kc, bb * 128:(bb + 1) * 128], CONDB[:, kc, :],
                           kc == 0, kc == 7, [wres, r_condb], [banks[b]], inc=(kc == 7))
                    blks.append((blk, col))
                    col += 2
                wrel(uu)
            blk0 = blks[0][0]
            nb = len(blks)
            for k in range(2):
                vtt(MODT[:, blk0:blk0 + nb, k], bank_ap(b)[:, 0:2 * nb].rearrange("p (b k) -> p b k", k=2)[:, :, k],
                    BADA[:, blk0:blk0 + nb], ALU.add, [banks[b], r_const], [r_mod])

        def derive(i):
            kind = i % 3
            n = i // 3
            srcv = MODT[:, i * 8:(i + 1) * 8, :]
            if kind == 0:
                vcopy(DER[:, i, :, :], srcv, [r_mod], [r_der[i]])
            elif kind == 1:
                for k in range(2):
                    vstt(DER[:, i, :, k], MODT[:, i * 8:(i + 1) * 8, k], 1.0, GN[:, n, :], ALU.add, ALU.mult,
                         [r_mod, r_const], [r_der[i]])
            else:
                vts(DER[:, i, :, :], srcv, 1.0 if n == 1 else 0.5, ALU.mult, [r_mod], [r_der[i]])

        ada_units(list(range(0, 8)))
        for i in range(2):
            derive(i)

        def ptt(out, in0, in1, op, reads, writes):
            return P.emit("pool", lambda e: e.tensor_tensor(out=out, in0=in0, in1=in1, op=op),
                          reads=reads, writes=writes)

        def pts(out, in0, s1, reads, writes):
            return P.emit("pool", lambda e: e.tensor_scalar(out=out, in0=in0, scalar1=s1, scalar2=None, op0=ALU.mult),
                          reads=reads, writes=writes)

        norm_cnt = [0]

        def norm_tile(nidx, tile, tmp=None, part="ab", pool_ok=True):
            ib = nidx * 3
            ia = nidx * 3 + 1
            t0, n, cond = tile
            if tmp is None:
                TMPA_, TMPB_, RS_, tag = TMPA, TMPB, RSTD[:, t0:t0 + n], "f"
            else:
                TMPA_, TMPB_, RSl, tag = tmp
                RS_ = RSl[:, (t0 // 512) % 2, 0:n]
            if part in ("a", "ab"):
                for c in range(8):
                    if c % 2 == 0 and pool_ok:
                        ptt(XN[:, c, t0:t0 + n], xview(c, t0, n), xview(c, t0, n), ALU.mult, [rx(c, t0)], [rxn(c, t0)])
                    else:
                        actv(XN[:, c, t0:t0 + n], xview(c, t0, n), AF.Square, [rx(c, t0)], [rxn(c, t0)])
            if part == "a":
                return
            b = poolB.next()
            for c in range(8):
                mm(bank_ap(b)[:, 0:n], ONES, XN[:, c, t0:t0 + n], c == 0, c == 7,
                   [rxn(c, t0), r_ones], [banks[b]], inc=(c == 7))
            s = norm_cnt[0] % 2
            norm_cnt[0] += 1
            rt = P.R("tmpa", tag, s)
            actv(TMPA_[:, s, 0:n], bank_ap(b)[:, 0:n], AF.Ln, [banks[b], r_eps], [rt], bias=EPSC, scale=1.0)
            rr = P.R("rstd", tag, t0 // 512)
            actv(RS_, TMPA_[:, s, 0:n], AF.Exp, [rt], [rr], scale=-0.5)
            for c in range(8):
                s2 = c % 2
                rb = P.R("tmpb", tag, s2)
                vstt(TMPB_[:, s2, 0:n], xview(c, t0, n), DER[:, ia, c, cond:cond + 1], RS_,
                     ALU.mult, ALU.mult, [rx(c, t0), rr, r_der[ia]], [rb])
                actv(XN[:, c, t0:t0 + n], TMPB_[:, s2, 0:n], AF.Identity, [rb, r_der[ib]], [rxn(c, t0)],
                     bias=DER[:, ib, c, cond:cond + 1], scale=1.0)

        def norm_phase(nidx, tiles):
            for tile in tiles:
                norm_tile(nidx, tile)

        hb_res = [[P.R("hb", s, j) for j in range(2)] for s in range(2)]
        sil_res = [P.R("sil", s) for s in range(2)]
        silc = [0]

        def ff_GU(G, U, rg, ru, nj, tile, hs, jjs=None):
            t0, n, cond = tile
            for jj in (range(nj) if jjs is None else jjs):
                ba = poolA.next()
                bu = poolA.next()
                for kc in range(8):
                    mm(bank_ap(ba)[:, 0:n], G[:, kc, jj * 128:(jj + 1) * 128], XN[:, kc, t0:t0 + n],
                       kc == 0, kc == 7, [rg, rxn(kc, t0)], [banks[ba]], inc=(kc == 7))
                for kc in range(8):
                    mm(bank_ap(bu)[:, 0:n], U[:, kc, jj * 128:(jj + 1) * 128], XN[:, kc, t0:t0 + n],
                       kc == 0, kc == 7, [ru, rxn(kc, t0)], [banks[bu]], inc=(kc == 7))
                ss = silc[0] % 2
                silc[0] += 1
                actv(SIL[:, ss, 0:n], bank_ap(ba)[:, 0:n], AF.Silu, [banks[ba]], [sil_res[ss]])
                vtt(HB[:, hs, jj, 0:n], SIL[:, ss, 0:n], bank_ap(bu)[:, 0:n], ALU.mult,
                    [sil_res[ss], banks[bu]], [hb_res[hs][jj]])

        def ff_DN(Dn, rd, nj, tile, hs, ig, ms=range(8)):
            t0, n, cond = tile
            for m in ms:
                by = poolDN.next()
                for jj in range(nj):
                    mm(bank_ap(by)[:, 0:n], Dn[:, jj, m * 128:(m + 1) * 128], HB[:, hs, jj, 0:n],
                       jj == 0, jj == nj - 1, [rd, hb_res[hs][jj]], [banks[by]], inc=(jj == nj - 1))
                vstt(xview(m, t0, n), bank_ap(by)[:, 0:n], DER[:, ig, m, cond:cond + 1], xview(m, t0, n),
                     ALU.mult, ALU.add, [banks[by], r_der[ig]], [rx(m, t0)])

        poolDN = PsumPool(P, [4, 5, 6, 7], "DN")

        def ff_phase(which, nidx, tiles, after_block=None, tile_done_a=None, tile_done_b=None, norm_done=(),
                     lazy_load=None, blk0_hook=None, defer_last=False):
            ig = nidx * 3 + 2
            ntt = len(tiles)
            for ti in range(ntt):
                if ti not in norm_done and (lazy_load is None or ti < 2):
                    norm_tile(nidx, tiles[ti], part="a", pool_ok=not (which == 0 and ti == 0))
            if 0 not in norm_done:
                norm_tile(nidx, tiles[0], part="b")
            nb = len(ff_blocks)
            for bi, (j0, nj) in enumerate(ff_blocks):
                ug, G, rg = wacq(("ffg", which, bi))
                uu, U, ru = wacq(("ffu", which, bi))
                ud, Dn, rd = wacq(("ffd", which, bi))
                late = []
                for step in range(ntt + 1):
                    if bi == 0 and lazy_load is not None and step + 2 < ntt:
                        lazy_load(step + 2)
                        norm_tile(nidx, tiles[step + 2], part="a")
                    if bi == 0 and step + 1 < ntt and (step + 1) not in norm_done:
                        norm_tile(nidx, tiles[step + 1], part="b")
                    if step < ntt:
                        ff_GU(G, U, rg, ru, nj, tiles[step], step % 2, jjs=[0])
                    if step >= 1:
                        ff_DN(Dn, rd, nj, tiles[step - 1], (step - 1) % 2, ig, ms=range(0, 4))
                    if step < ntt:
                        ff_GU(G, U, rg, ru, nj, tiles[step], step % 2, jjs=range(1, nj))
                    if bi == 0 and step == 0 and blk0_hook is not None:
                        blk0_hook()
                    if step == ntt:
                        wrel(ug)
                        wrel(uu)
                    if bi == nb - 1 and tile_done_b is not None:
                        for ti in late:
                            tile_done_b(ti)
                        late = []
                    if step >= 1:
                        ff_DN(Dn, rd, nj, tiles[step - 1], (step - 1) % 2, ig, ms=range(4, 8))
                        if bi == nb - 1:
                            if tile_done_a is not None:
                                tile_done_a(step - 1)
                            late.append(step - 1)
                wrel(ud)
                leftover = []
                if bi == nb - 1 and tile_done_b is not None:
                    if defer_last:
                        leftover = list(late)
                    else:
                        for ti in late:
                            tile_done_b(ti)
                if after_block is not None:
                    after_block(bi)
            return leftover

        ocnt = [0]
        out_toks = []

        def out_tile(ti):
            t0, n, cond = TT_OWN[ti]
            for sub in range(n // 128):
                tk = t0 + sub * 128
                s = ocnt[0] % 4
                ocnt[0] += 1
                for half in range(2):
                    b = poolC.next()
                    for cc in range(4):
                        c = half * 4 + cc
                        tr(bank_ap(b)[:, cc * 128:(cc + 1) * 128], X[:, c, tk:tk + 128],
                           [rx(c, tk)], [banks[b]], inc=(cc == 3))
                    if half == 0:
                        actv(STG[:, s, 0:512], bank_ap(b)[:, :], AF.Copy, [banks[b]], [stg_res[s][0]])
                    else:
                        vcopy(STG[:, s, 512:1024], bank_ap(b)[:, :], [banks[b]], [stg_res[s][1]])
                tok = sp_dma(yo[tk:tk + 128, :], STG[:, s, :], "stg%d" % s, reads=stg_res[s])
                out_toks.append(tok)


        def mixer_half(half, q_hooks=()):
            is_s = (half == 1)
            q_hooks = list(q_hooks)
            own = TT_ALL[2:4] if is_s else TT_ALL[0:2]
            kvt = own + ([TT_ALL[4]] if is_s else [])
            base = 1024 if is_s else 0
            nkv = 1280 if is_s else 1024
            cond = 1 if is_s else 0

            def lc(t0):
                return t0 - base

            if is_s:
                P.barrier()
            A.reset(mix0)
            ATT = A.take(4 * 1024 * 2, BF16, "p (h t) -> p h t", h=4)
            m1 = A.mark()
            KT = A.take(4 * nkv * 2, BF16, "p (h t) -> p h t", h=4)
            VB = A.take((nkv // 128) * 512 * 2, BF16, "p (i f) -> p i f", f=512)
            if is_s:
                KCT = A.take(4 * 512 * 2, BF16, "p (h t) -> p h t", h=4)
                VC = A.take(4 * 512 * 2, BF16, "p (i f) -> p i f", f=512)
            else:
                A.take(1024, F32)
                assert (A.mark() - mix0) * 4 >= 25600
            QT = A.take(4 * 1024 * 2, BF16, "p (h t) -> p h t", h=4)
            m2 = A.mark()
            SQ = A.take(2 * 512 * 2, BF16, "p (s t) -> p s t", s=2)
            TA1 = A.take(2 * 512 * 4, F32, "p (s t) -> p s t", s=2)
            TB1 = A.take(2 * 512 * 4, F32, "p (s t) -> p s t", s=2)
            if is_s:
                STGC = A.take(4 * 512 * 4, F32, "p (s t) -> p s t", s=4)
            else:
                KNF2 = A.take(2 * 4 * 512 * 4, F32, "p (s h t) -> p s h t", s=2, h=4)
                OST = A.take(2 * 512 * 4, F32, "p (s t) -> p s t", s=2)

            r_qt = [P.R("qt", half, hp) for hp in range(4)]
            r_kt = [P.R("kt", half, hp) for hp in range(4)]
            r_vb = [P.R("vb", half, i) for i in range(nkv // 128)]
            r_att = [P.R("att", half, hp) for hp in range(4)]
            r_sq = [P.R("sq", s) for s in range(2)]
            r_ta1 = [P.R("ta1", s) for s in range(2)]
            r_tb1 = [P.R("tb1", s) for s in range(2)]
            cnt = {"n": 0, "q": 0}

            def qk_proj(nm, tiles, dst, r_dst, gain_col):
                us = [wacq(("win", half, nm, u)) for u in range(2)]
                pend = []
                want_nk = (nm == "k" and not is_s)

                def finish(item):
                    b, s, hp, t0, n, slot = item
                    b2 = poolB.next()
                    mm(bank_ap(b2)[:, 0:n], ONESBLK, SQ[:, s, 0:n], True, True, [r_sq[s], r_mc], [banks[b2]], inc=True)
                    actv(TA1[:, s, 0:n], bank_ap(b2)[:, 0:n], AF.Ln, [banks[b2], r_eps], [r_ta1[s]],
                         bias=EPSC, scale=1.0)
                    actv(TB1[:, s, 0:n], TA1[:, s, 0:n], AF.Exp, [r_ta1[s]], [r_tb1[s]], scale=-0.5)
                    if want_nk:
                        rk = P.R("knf", slot, hp)
                        vstt(KNF2[:, slot, hp, 0:n], bank_ap(b)[:, 0:n], gain_col, TB1[:, s, 0:n], ALU.mult, ALU.mult,
                             [banks[b], r_tb1[s], r_mc, r_const], [rk])
                        vcopy(dst[:, hp, lc(t0):lc(t0) + n], KNF2[:, slot, hp, 0:n], [rk], [r_dst[hp]])
                    else:
                        vstt(dst[:, hp, lc(t0):lc(t0) + n], bank_ap(b)[:, 0:n], gain_col, TB1[:, s, 0:n],
                             ALU.mult, ALU.mult, [banks[b], r_tb1[s], r_mc, r_const], [r_dst[hp]])

                def emit_nk(t0, slot):
                    for sub in range(4):
                        b3 = poolC.next()
                        for hp in range(4):
                            tr(bank_ap(b3)[:, hp * 128:(hp + 1) * 128], KNF2[:, slot, hp, sub * 128:(sub + 1) * 128],
                               [P.R("knf", slot, hp)], [banks[b3]], inc=(hp == 3))
                        so = cnt["n"] % 2
                        cnt["n"] += 1
                        ro = P.R("ost", so)
                        vcopy(OST[:, so, :], bank_ap(b3)[:, :], [banks[b3]], [ro])
                        tk = t0 + sub * 128
                        out_toks.append(sp_dma(nk_d[tk:tk + 128, :], OST[:, so, :], "ost%d" % so, reads=[ro]))

                nk_pending = []
                for ti_, (t0, n, _c) in enumerate(tiles):
                    for u in range(2):
                        uu, W, rw = us[u]
                        for bb in range(2):
                            hp = 2 * u + bb
                            b = poolA.next()
                            for kc in range(8):
                                mm(bank_ap(b)[:, 0:n], W[:, kc, bb * 128:(bb + 1) * 128], XN[:, kc, t0:t0 + n],
                                   kc == 0, kc == 7, [rw, rxn(kc, t0)], [banks[b]], inc=(kc == 7))
                            s = cnt["q"] % 2
                            cnt["q"] += 1
                            actv(SQ[:, s, 0:n], bank_ap(b)[:, 0:n], AF.Square, [banks[b]], [r_sq[s]])
                            pend.append((b, s, hp, t0, n, ti_ % 2))
                            if len(pend) > 1:
                                finish(pend.pop(0))
                    if nm == "q" and q_hooks:
                        q_hooks.pop(0)()
                    if want_nk:
                        for (pt0, pslot) in nk_pending:
                            emit_nk(pt0, pslot)
                        nk_pending = [(t0, ti_ % 2)]
                while pend:
                    finish(pend.pop(0))
                tail = [(lambda a=pt0, b_=pslot: emit_nk(a, b_)) for (pt0, pslot) in nk_pending]
                for uu, W, rw in us:
                    wrel(uu)
                return tail

            qk_proj("q", own, QT, r_qt, QG)
            while q_hooks:
                q_hooks.pop(0)()
            if is_s:
                r_kct = P.R("kct")
                r_vc = P.R("vc")
                P.emit("pool", lambda e: e.dma_start(out=VC, in_=cv_d.rearrange("(i p) f -> p i f", p=128)),
                       writes=[r_vc], dma_sem="vcl")
                r_stgc = [P.R("stgc", s) for s in range(4)]
                for i in range(4):
                    s = i
                    sp_dma(STGC[:, s, :], ck_d[i * 128:(i + 1) * 128, :], "stgc%d" % s, writes=[r_stgc[s]])
                    b = poolC.next()
                    for hp in range(4):
                        tr(bank_ap(b)[:, hp * 128:(hp + 1) * 128], STGC[:, s, hp * 128:(hp + 1) * 128],
                           [r_stgc[s]], [banks[b]], inc=(hp == 3))
                    actv(KCT[:, :, i * 128:(i + 1) * 128], bank_ap(b)[:, :].rearrange("p (h t) -> p h t", h=4),
                         AF.Copy, [banks[b]], [r_kct])

            if LV[stops[half]] < 0.6:
                return
            if not is_s:
                P.barrier()
            nk_tail = qk_proj("k", kvt, KT, r_kt, KG)
            if LV[stops[half]] < 1:
                for f_ in nk_tail:
                    f_()
                return

            P.barrier()
            A.reset(m2)
            NT = 10
            PRW = A.take(2 * NT * 128 * 2, BF16, "p (s t) -> p s t", s=2)
            RC = A.take(2 * (128 if is_s else 256) * 4, F32, "p (s t) -> p s t", s=2)
            r_prw = [[[P.R("prw", s, k, q) for q in range(2)] for k in range(NT)] for s in range(2)]
            r_rc = [P.R("rc", s) for s in range(2)]
            poolS = PsumPool(P, [0, 1, 2, 3, 4, 5], "S")
            poolAcc = PsumPool(P, [6, 7], "Acc")
            if not is_s:
                VST = A.take(2 * 512 * 4, F32, "p (s t) -> p s t", s=2)
            if is_s:
                EBST = A.take(2 * 14 * 64 * 4, F32)
                EBH = A.take(2 * 14 * 64 * 2, BF16, "p (h s c) -> p h s c", h=2, s=14)
                EBI = A.take(2 * 10 * 64 * 2, BF16, "p (h a two c) -> p h a two c", h=2, a=5, two=2)
                r_ebst = P.R("ebst")
                r_ebh = P.R("ebh")
                r_ebi = P.R("ebi")
                tiles = qb_tiles()
                NQ = 128
            else:
                NQ = 256

            def eb_load(hp):
                sp_dma(EBST, eb2_d[:, hp * 1792:(hp + 1) * 1792], "ebst", writes=[r_ebst])
                actv(EBST, EBST, AF.Exp, [r_ebst], [r_ebst])

            def eb_build_h(a0=0, a1=28):
                for a in range(a0, a1):
                    vtt(EBH[:, a // 14, a % 14, :], EBST[:, a * 64:(a + 1) * 64], CMASK, ALU.mult,
                        [r_ebst, r_const], [r_ebh])

            def eb_build_i():
                ebi_flat = EBI.rearrange("p h a two c -> p h (a two c)")
                vcopy(ebi_flat, EBH[:, :, 2:12, :].rearrange("p h s c -> p h (s c)"), [r_ebh], [r_ebi])
                vmemset(ebi_flat[0:64, :, 0:64], 0.0, [r_ebi])
                vmemset(ebi_flat[64:128, :, 8 * 64:9 * 64], 0.0, [r_ebi])
                vmemset(ebi_flat[:, :, 9 * 64:10 * 64], 0.0, [r_ebi])

            if is_s:
                eb_load(0)
                eb_build_h()
                eb_build_i()
            us = [wacq(("win", half, "v", u)) for u in range(2)]
            for i in range(nkv // 128):
                tk = base + i * 128
                sv = i % 2
                rv = P.R("vst", sv)
                for u in range(2):
                    uu, W, rw = us[u]
                    b = poolA.next()
                    for kc in range(8):
                        mm(bank_ap(b)[:, 0:256], XN[:, kc, tk:tk + 128], W[:, kc, :], kc == 0, kc == 7,
                           [rw, rxn(kc, tk)], [banks[b]], inc=(kc == 7))
                    if is_s:
                        actv(VB[:, i, u * 256:(u + 1) * 256], bank_ap(b)[:, 0:256], AF.Copy, [banks[b]], [r_vb[i]])
                    else:
                        rvu = P.R("vst", sv, u)
                        actv(VST[:, sv, u * 256:(u + 1) * 256], bank_ap(b)[:, 0:256], AF.Copy, [banks[b]], [rvu])
                        vcopy(VB[:, i, u * 256:(u + 1) * 256], VST[:, sv, u * 256:(u + 1) * 256], [rvu], [r_vb[i]])
                if not is_s:
                    out_toks.append(sp_dma(nv_d[tk:tk + 128, :], VST[:, sv, :], "vst%d" % sv,
                                           reads=[P.R("vst", sv, 0), P.R("vst", sv, 1)]))
            for uu, W, rw in us:
                wrel(uu)
            for f_ in nk_tail:
                f_()

            if LV[stops[half]] < 2:
                return
            def unit_list(hp):
                ul = []
                if not is_s:
                    for sq in range(4):
                        tq = sq * 256
                        kl = []
                        for kc in range(2):
                            vt = 2 * sq + kc
                            kl.append((KT, tq + kc * 128, r_kt[hp], VB, vt, r_vb[vt], None))
                        ul.append((tq, kl))
                else:
                    vmi = 0
                    for b in range(8):
                        kl = [(KCT, i * 128, r_kct, VC, i, r_vc, None) for i in range(4)]
                        for ti, (kind, idx, delta) in enumerate(tiles[b]):
                            kcol = idx * 128 if kind == "o" else 1024 + idx * 128
                            vt = idx if kind == "o" else 8 + idx
                            kl.append((KT, kcol, r_kt[hp], VB, vt, r_vb[vt], (delta, vmi + ti)))
                        vmi += len(tiles[b])
                        ul.append((b * 128, kl))
                    assert vmi == NVM
                return ul

            ucount = {"n": 0, "r": 0}

            def emit_S(hp, h, tq, kl):
                pr = slice(h * 64, (h + 1) * 64)
                interior = is_s and (2 <= tq // 128 <= 5)
                if interior:
                    assert len(kl) == 9 and [m[6][0] for m in kl[4:]] == [-2, -1, 0, 1, 2]
                ps = ucount["n"] % 2
                ucount["n"] += 1
                per_bank = 512 // NQ
                for g0 in range(0, len(kl), per_bank):
                    grp = kl[g0:g0 + per_bank]
                    sb = poolS.next()
                    for k, (Ksrc, kcol, rk, _V, _vt, _rv, _m) in enumerate(grp):
                        mm(bank_ap(sb)[:, k * NQ:(k + 1) * NQ], Ksrc[pr, hp, kcol:kcol + 128], QT[pr, hp, tq:tq + NQ],
                           True, True, [rk, r_qt[hp]], [banks[sb]], inc=(k == len(grp) - 1))
                    wr = []
                    for k in range(len(grp)):
                        wr += r_prw[ps][g0 + k]
                    actv(PRW[:, ps, g0 * NQ:(g0 + len(grp)) * NQ], bank_ap(sb)[:, 0:len(grp) * NQ], AF.Exp,
                         [banks[sb]], wr)
                    for k, (_K, _kc, _rk, _V, _vt, _rv, minfo) in enumerate(grp):
                        if minfo is None or interior:
                            continue
                        delta, vi = minfo
                        for jq in range(2):
                            sidx = 2 * delta + 7 - jq
                            assert 0 <= sidx < 14
                            c0 = (g0 + k) * 128 + jq * 64
                            sl = PRW[:, ps, c0:c0 + 64]
                            vstt(sl, sl, VM[:, 2 * vi + jq:2 * vi + jq + 1], EBH[:, h, sidx, :],
                                 ALU.mult, ALU.mult, [r_ebh, r_const], [r_prw[ps][g0 + k][jq]])
                if interior:
                    pv = PRW[:, ps, :].rearrange("p (k q c) -> p k q c", k=NT, q=2)
                    for jq in range(2):
                        wr = [r_prw[ps][4 + k][jq] for k in range(5)]
                        vtt(pv[:, 4:9, jq, :], pv[:, 4:9, jq, :], EBI[:, h, :, 1 - jq, :], ALU.mult, [r_ebi], wr)
                return ps

            def emit_PV(hp, h, tq, kl, ps, acc):
                pr = slice(h * 64, (h + 1) * 64)
                nk = len(kl)
                for k, (_K, _kc, _rk, Vsrc, vt, rv, _m) in enumerate(kl):
                    rd = r_prw[ps][k] if NQ == 128 else (r_prw[ps][2 * k] + r_prw[ps][2 * k + 1])
                    mm(bank_ap(acc)[pr, 0:NQ], Vsrc[:, vt, hp * 128 + h * 64:hp * 128 + (h + 1) * 64],
                       PRW[:, ps, k * NQ:(k + 1) * NQ], k == 0, False, list(rd) + [rv], [banks[acc]], inc=False, sgc=True)
                    mm(bank_ap(acc)[pr, NQ:2 * NQ], ONESB, PRW[:, ps, k * NQ:(k + 1) * NQ], False, k == nk - 1,
                       list(rd) + [r_mc], [banks[acc]], inc=(k == nk - 1), sgc=True)

            def emit_norm(hp, tq, acc):
                rs = ucount["r"] % 2
                ucount["r"] += 1
                actv(RC[:, rs, 0:NQ], bank_ap(acc)[:, NQ:2 * NQ], AF.Ln, [banks[acc]], [r_rc[rs]])
                actv(RC[:, rs, 0:NQ], RC[:, rs, 0:NQ], AF.Exp, [r_rc[rs]], [r_rc[rs]], scale=-1.0)
                vtt(ATT[:, hp, tq:tq + NQ], bank_ap(acc)[:, 0:NQ], RC[:, rs, 0:NQ], ALU.mult,
                    [banks[acc], r_rc[rs]], [r_att[hp]])

            for hp in range(4):
                ul = unit_list(hp)
                if is_s:
                    ul = [ul[b] for b in (0, 1, 6, 7, 2, 3, 4, 5)]
                    if hp < 3:
                        eb_load(hp + 1)
                units = []
                for (tq, kl) in ul:
                    for h in range(2):
                        units.append((h, tq, kl))
                pend = None
                accs = {}
                for ui, (h, tq, kl) in enumerate(units):
                    ps = emit_S(hp, h, tq, kl)
                    if is_s and hp < 3 and 7 <= ui < 14:
                        eb_build_h(4 * (ui - 7), 4 * (ui - 6))
                    if pend is not None:
                        ph, ptq, pkl, pps = pend
                        if ph == 0:
                            accs[ptq] = poolAcc.next()
                        emit_PV(hp, ph, ptq, pkl, pps, accs[ptq])
                        if ph == 1:
                            emit_norm(hp, ptq, accs[ptq])
                    pend = (h, tq, kl, ps)
                if is_s and hp < 3:
                    eb_build_i()
                ph, ptq, pkl, pps = pend
                emit_PV(hp, ph, ptq, pkl, pps, accs[ptq])
                emit_norm(hp, ptq, accs[ptq])

            if LV[stops[half]] < 3:
                return
            P.barrier()
            A.reset(m1)
            YT = A.take(4 * 1024 * 2, BF16, "p (g t) -> p g t", g=4)
            m3 = A.mark()
            NS, L = (1, 1024) if is_s else (4, 256)
            LP = L + 16
            PP = A.take(4 * NS * LP * 4, F32, "p (g s l) -> p g s l", g=4, s=NS)
            TA = A.take(NS * LP * 4, F32, "p (s l) -> p s l", s=NS)
            TB = A.take(NS * LP * 4, F32, "p (s l) -> p s l", s=NS)
            TA2, TB2 = TA, TB
            DT = A.take(4 * 1024 * 2, BF16, "p (g t) -> p g t", g=4)
            r_pp = [P.R("pp", g) for g in range(4)]
            r_ta, r_tb = P.R("ta"), P.R("tb")
            r_dt = [P.R("dt", g) for g in range(4)]
            r_yt = [P.R("yt", g) for g in range(4)]
            for g in range(4):
                vmemset(PP[:, g, :, 0:8], 0.0, [r_pp[g]])
                vmemset(PP[:, g, :, L + 8:L + 16], 0.0, [r_pp[g]])
            us = [wacq(("win", half, "p", u)) for u in range(2)]
            for u in range(2):
                uu, W, rw = us[u]
                for bb in range(2):
                    g = 2 * u + bb
                    for ti, (t0, n, _c) in enumerate(own):
                        b = poolA.next()
                        for kc in range(8):
                            mm(bank_ap(b)[:, 0:n], W[:, kc, bb * 128:(bb + 1) * 128], XN[:, kc, t0:t0 + n],
                               kc == 0, kc == 7, [rw, rxn(kc, t0)], [banks[b]], inc=(kc == 7))
                        if is_s:
                            actv(PP[:, g, 0, 8 + ti * 512:8 + (ti + 1) * 512], bank_ap(b)[:, 0:512], AF.Copy,
                                 [banks[b]], [r_pp[g]])
                        else:
                            actv(PP[:, g, 2 * ti:2 * ti + 2, 8:8 + 256],
                                 bank_ap(b)[:, 0:512].rearrange("p (s l) -> p s l", s=2), AF.Copy, [banks[b]], [r_pp[g]])
                    if is_s:
                        t0, n, _c = TT_ALL[4]
                        b = poolA.next()
                        for kc in range(8):
                            mm(bank_ap(b)[:, 0:n], W[:, kc, bb * 128:(bb + 1) * 128], XN[:, kc, t0:t0 + n],
                               kc == 0, kc == 7, [rw, rxn(kc, t0)], [banks[b]], inc=(kc == 7))
                        actv(PP[:, g, 0, 0:8], bank_ap(b)[:, 248:256], AF.Identity, [banks[b], r_const], [r_pp[g]],
                             scale=FLG[:, 0:1])
                        actv(PP[:, g, 0, L + 8:L + 16], bank_ap(b)[:, 0:8], AF.Identity, [banks[b], r_const], [r_pp[g]],
                             scale=FLG[:, 1:2])
            for uu, W, rw in us:
                wrel(uu)
            EDGE = EDGE_S if is_s else EDGE_P
            for g in range(4):
                on_pool = False
                ett = ptt if on_pool else vtt
                cur = PP[:, g]
                TA_, TB_ = (TA2, TB2) if on_pool else (TA, TB)
                r_ta_, r_tb_ = (P.R("ta2"), P.R("tb2")) if on_pool else (r_ta, r_tb)
                ett(TA_[:, :, 1:LP], cur[:, :, 0:LP - 1], cur[:, :, 1:LP], ALU.add, [r_pp[g]], [r_ta_])
                wb, rwb = TA_, r_ta_
                if g >= 1:
                    ett(TB_[:, :, 2:LP - 1], TA_[:, :, 1:LP - 2], TA_[:, :, 3:LP], ALU.add, [r_ta_], [r_tb_])
                    wb, rwb = TB_, r_tb_
                if g >= 2:
                    ett(TA_[:, :, 4:LP - 3], TB_[:, :, 2:LP - 5], TB_[:, :, 6:LP - 1], ALU.add, [r_tb_], [r_ta_])
                    wb, rwb = TA_, r_ta_
                if g >= 3:
                    ett(TB_[:, :, 8:LP - 7], TA_[:, :, 4:LP - 11], TA_[:, :, 12:LP - 3], ALU.add, [r_ta_], [r_tb_])
                    wb, rwb = TB_, r_tb_
                for s in range(NS):
                    ett(wb[:, s, 8:16], wb[:, s, 8:16], EDGE[:, g, 0:8], ALU.mult, [rwb, r_const], [rwb])
                    ett(wb[:, s, L:L + 8], wb[:, s, L:L + 8], EDGE[:, g, 8:16], ALU.mult, [rwb, r_const], [rwb])
                wwin = (2, 4, 8, 16)[g]
                if on_pool:
                    pts(wb[:, :, 8:L + 8], wb[:, :, 8:L + 8], 1.0 / wwin, [rwb], [rwb])
                    ptt(DT[:, g, :].rearrange("p (s l) -> p s l", s=NS), wb[:, :, 8:L + 8], cur[:, :, 8:L + 8],
                        ALU.subtract, [rwb, r_pp[g]], [r_dt[g]])
                else:
                    vstt(DT[:, g, :].rearrange("p (s l) -> p s l", s=NS), wb[:, :, 8:L + 8], 1.0 / wwin,
                         cur[:, :, 8:L + 8], ALU.mult, ALU.subtract, [rwb, r_pp[g]], [r_dt[g]])
            for g in range(4):
                for ti in range(2):
                    b = poolA.next()
                    mm(bank_ap(b)[:, 0:512], WPOOL[:, g, :], DT[:, g, ti * 512:(ti + 1) * 512], True, True,
                       [r_wpool, r_dt[g]], [banks[b]], inc=True)
                    actv(YT[:, g, ti * 512:(ti + 1) * 512], bank_ap(b)[:, 0:512], AF.Identity, [banks[b], r_const],
                         [r_yt[g]], scale=PSC[:, g:g + 1])

            if LV[stops[half]] < 4:
                return
            P.barrier()
            A.reset(m3)
            MRG = A.take(8 * 1024 * 2, BF16, "p (j t) -> p j t", j=8)
            SG = A.take(4 * 512 * 4, F32, "p (s t) -> p s t", s=4)
            TM = A.take(4 * 512 * 4, F32, "p (s t) -> p s t", s=4)
            r_mrg = [P.R("mrg", j) for j in range(8)]
            r_sg = [P.R("sg", s) for s in range(4)]
            r_tm = [P.R("tm", s) for s in range(4)]
            mc = {"n": 0}
            poolM = PsumPool(P, [0, 1, 2, 3, 4, 5, 6, 7], "M")
            if is_s and stops[1] == "merge":
                NTA = A.take(2 * 512 * 4, F32, "p (s t) -> p s t", s=2)
                NTB = A.take(2 * 512 * 4, F32, "p (s t) -> p s t", s=2)
                NRS = A.take(2 * 512 * 4, F32, "p (s t) -> p s t", s=2)
                for tile in TT_ALL[0:2]:
                    norm_tile(2, tile, tmp=(NTA, NTB, NRS, "m"))
            for jp in range(4):
                u0 = wacq(("g0", half, jp))
                u1 = wacq(("g1", half, jp))
                u2 = wacq(("brp", half, jp))
                u3 = wacq(("bra", half, jp))
                for bb in range(2):
                    j = 2 * jp + bb
                    cs = slice(bb * 128, (bb + 1) * 128)
                    for ti, (t0, n, _c) in enumerate(own):
                        tl = slice(ti * 512, (ti + 1) * 512)
                        ba, bbk, bg0, bg1 = poolM.next(), poolM.next(), poolM.next(), poolM.next()
                        for kc in range(4):
                            mm(bank_ap(ba)[:, :], u2[1][:, kc, cs], YT[:, kc, tl], kc == 0, kc == 3,
                               [u2[2], r_yt[kc]], [banks[ba]], inc=(kc == 3))
                        for kc in range(4):
                            mm(bank_ap(bbk)[:, :], u3[1][:, kc, cs], ATT[:, kc, tl], kc == 0, kc == 3,
                               [u3[2], r_att[kc]], [banks[bbk]], inc=(kc == 3))
                        for kc in range(8):
                            mm(bank_ap(bg0)[:, :], u0[1][:, kc, cs], XN[:, kc, t0:t0 + 512], kc == 0, kc == 7,
                               [u0[2], rxn(kc, t0)], [banks[bg0]], inc=(kc == 7))
                        for kc in range(8):
                            mm(bank_ap(bg1)[:, :], u1[1][:, kc, cs], XN[:, kc, t0:t0 + 512], kc == 0, kc == 7,
                               [u1[2], rxn(kc, t0)], [banks[bg1]], inc=(kc == 7))
                        s0 = (mc["n"] % 2) * 2
                        mc["n"] += 1
                        actv(SG[:, s0, :], bank_ap(bg0)[:, :], AF.Sigmoid, [banks[bg0]], [r_sg[s0]])
                        actv(SG[:, s0 + 1, :], bank_ap(bg1)[:, :], AF.Sigmoid, [banks[bg1]], [r_sg[s0 + 1]])
                        vtt(TM[:, s0, :], SG[:, s0, :], bank_ap(ba)[:, :], ALU.mult, [r_sg[s0], banks[ba]], [r_tm[s0]])
                        vtt(TM[:, s0 + 1, :], SG[:, s0 + 1, :], bank_ap(bbk)[:, :], ALU.mult,
                            [r_sg[s0 + 1], banks[bbk]], [r_tm[s0 + 1]])
                        vtt(MRG[:, j, tl], TM[:, s0, :], TM[:, s0 + 1, :], ALU.add, [r_tm[s0], r_tm[s0 + 1]], [r_mrg[j]])
                for uq in (u0, u1, u2, u3):
                    wrel(uq[0])
            for mu in range(4):
                uo = wacq(("wo", half, mu))
                for bb in range(2):
                    m = 2 * mu + bb
                    for ti, (t0, n, _c) in enumerate(own):
                        by = poolB.next()
                        for j in range(8):
                            mm(bank_ap(by)[:, :], uo[1][:, j, bb * 128:(bb + 1) * 128], MRG[:, j, ti * 512:(ti + 1) * 512],
                               j == 0, j == 7, [uo[2], r_mrg[j]], [banks[by]], inc=(j == 7))
                        vstt(X[:, m, t0:t0 + 512], bank_ap(by)[:, :], DER[:, 5, m, cond:cond + 1], X[:, m, t0:t0 + 512],
                             ALU.mult, ALU.add, [banks[by], r_der[5]], [rx(m, t0)])
                wrel(uo[0])

        ada_state = {"u": 12}

        def ff1_after_block(bi):
            us = []
            for _ in range(3):
                if ada_state["u"] < 36:
                    us.append(ada_state["u"])
                    ada_state["u"] += 1
            if us:
                ada_units(us)
            if bi == 8:
                assert ada_state["u"] == 36
                for i in range(3, 9):
                    derive(i)

        def norm2_a(ti):
            if with_mixer:
                norm_tile(1, TT_ALL[ti], part="a")

        def norm2_b(ti):
            if with_mixer:
                norm_tile(1, TT_ALL[ti], part="b")

        def ff1_blk0_hook():
            ada_units(list(range(8, 12)))
            derive(2)

        full_mixer = with_mixer and stops[0] is not None
        left = ff_phase(0, 0, TT_ALL, after_block=ff1_after_block, tile_done_a=norm2_a, tile_done_b=norm2_b,
                        lazy_load=load_x_tile, blk0_hook=ff1_blk0_hook, defer_last=full_mixer)

        if with_mixer:
            if stops[0] is not None:
                mixer_half(0, q_hooks=[(lambda ti=ti: norm2_b(ti)) for ti in left])
            if stops[1] is not None:
                mixer_half(1)
            P.barrier()
            A.reset(ffend)

        ff_phase(1, 2, TT_OWN, tile_done_b=out_tile,
                 norm_done=(0, 1) if (with_mixer and stops[1] == "merge") else ())

        final_waits = {}
        for t in out_toks:
            final_waits[t.sem] = max(final_waits.get(t.sem, 0), t.val)
        assert wstate["next_acq"] == len(wsched), (wstate["next_acq"], len(wsched))

        block = es.enter_context(nc.Block())

        @block.tensor
        def _(e):
            P.replay("pe", e)

        @block.scalar
        def _(e):
            P.replay("act", e)

        @block.vector
        def _(e):
            P.replay("dve", e)

        @block.gpsimd
        def _(e):
            P.replay("pool", e)

        @block.sync
        def _(e):
            P.replay("sp", e, final_waits=list(final_waits.items()))

    return nc


_NC_CACHE = {}


def _get_nc(with_mixer=True):
    key = with_mixer
    if key not in _NC_CACHE:
        _NC_CACHE[key] = build_program(with_mixer=with_mixer)
    return _NC_CACHE[key]


def _colT(v):
    return np.ascontiguousarray(np.asarray(v, np.float32).reshape(8, 128).T)


def make_in_maps(inp):
    f = lambda a: np.ascontiguousarray(np.asarray(a, np.float32))
    x_prompt = f(inp["x_prompt"])
    x_sample = f(inp["x_sample"])
    shared = {
        "w_ada": f(inp["w_ada"][0]),
        "b_adaT": np.ascontiguousarray(f(inp["b_ada"][0]).reshape(72, 128).T),
        "gnT": np.ascontiguousarray(np.concatenate(
            [_colT(inp["g_ff1"][0]), _colT(inp["g_mix"][0]), _colT(inp["g_ff2"][0])], axis=1)),
        "w_ff1_in": f(inp["w_ff1_in"][0]), "w_ff1_out": f(inp["w_ff1_out"][0]),
        "w_ff2_in": f(inp["w_ff2_in"][0]), "w_ff2_out": f(inp["w_ff2_out"][0]),
        "ident": np.eye(128, dtype=np.float32),
        "w_in": f(inp["w_in"][0]), "w_pool": f(inp["w_pool"][0]),
        "w_br_pool": f(inp["w_br_pool"][0]), "w_br_att": f(inp["w_br_att"][0]), "w_out": f(inp["w_out"][0]),
    }
    gainT = np.empty((128, 6), np.float32)
    gainT[:, 0] = np.tile(f(inp["q_gain"][0]), 2)
    gainT[:, 1] = np.tile(f(inp["k_gain"][0]), 2)
    gainT[:, 2:6] = f(inp["pool_scale"][0]).reshape(4, 128).T
    shared["gainT"] = gainT
    rpb = f(inp["rpb"][0])
    ii = np.arange(2)[:, None, None, None]
    ck = np.arange(64)[None, :, None, None]
    ss = np.arange(14)[None, None, :, None]
    cq = np.arange(64)[None, None, None, :]
    dr = np.broadcast_to(ss + ii, (2, 64, 14, 64))
    dc = np.broadcast_to(np.clip(ck - cq + 15, 0, 30), (2, 64, 14, 64))
    eb = rpb[:, dr, dc]
    shared["eb2src"] = np.ascontiguousarray(eb.transpose(1, 2, 0, 3, 4).reshape(128, 8 * 14 * 64))
    colq = np.arange(64)
    c0 = np.clip(colq - 8, 0, 48)
    ok = (colq[:, None] >= c0[None, :]) & (colq[:, None] < c0[None, :] + 16)
    shared["colmask"] = np.ascontiguousarray(np.tile(ok.astype(np.float32), (2, 1)))
    tiles = qb_tiles()

    def edge_tab(L, start_real, end_real):
        e = np.ones((4, 16), np.float32)
        for g, w in enumerate((2, 4, 8, 16)):
            for t in range(8):
                if start_real and t < w // 2:
                    e[g, t] = w / float(t + w // 2)
                te = L - 8 + t
                if end_real and te + w // 2 > L:
                    e[g, 8 + t] = w / float(L - te + w // 2)
        return e.reshape(64)
    maps = []
    for i in range(NCORES):
        b, par = i // 2, i % 2
        xp = x_prompt[4 * i:4 * i + 4].reshape(NTOK_P, D)
        own = x_sample[b, par * 1024:(par + 1) * 1024]
        halo = x_sample[b, 1024:1280] if par == 0 else x_sample[b, 768:1024]
        xin = np.ascontiguousarray(np.concatenate([xp, own, halo], axis=0))
        condT = np.empty((128, 8, 2), np.float32)
        condT[:, :, 0] = _colT(inp["c_ctx"])
        condT[:, :, 1] = _colT(inp["c"][b])
        cst = np.zeros((128, NCST), np.float32)
        cst[:, 0] = 1.0 if par == 1 else 0.0
        cst[:, 1] = 1.0 if par == 0 else 0.0
        cst[:, 2:66] = edge_tab(256, True, True)[None, :]
        cst[:, 66:130] = edge_tab(1024, par == 0, par == 1)[None, :]
        t = 0
        for bq in range(8):
            for (kind, idx, delta) in tiles[bq]:
                for i2 in range(2):
                    for jq in range(2):
                        rq = par * 16 + 2 * bq + jq
                        r0 = min(max(rq - 4, 0), 24)
                        if kind == "o":
                            rk = par * 16 + 2 * idx + i2
                        else:
                            rk = (16 if par == 0 else 12) + 2 * idx + i2
                        valid = (r0 <= rk <= r0 + 7) and (rk - rq == 2 * delta + i2 - jq)
                        cst[i2 * 64:(i2 + 1) * 64, 130 + 2 * t + jq] = 1.0 if valid else 0.0
                t += 1
        assert t == NVM
        m = dict(shared)
        m["cst"] = cst
        m["ck"] = np.ascontiguousarray(f(inp["cache_k"][b, 0]).reshape(512, 512))
        m["cv"] = np.ascontiguousarray(f(inp["cache_v"][b, 0]).reshape(512, 512))
        m["xin"] = xin
        m["condT"] = np.ascontiguousarray(condT.reshape(128, 16))
        maps.append(m)
    return maps


def kernel(**inputs):
    nc = _get_nc(True)
    maps = make_in_maps(inputs)
    res = run_bass_kernel_spmd(nc, maps, core_ids=list(range(NCORES)))
    B, S = 32, 256
    y_prompt = np.empty((B, S, D), np.float32)
    y_sample = np.empty((4, 2048, D), np.float32)
    new_k = np.empty((B, 1, S, 8, 64), np.float32)
    new_v = np.empty((B, 1, S, 8, 64), np.float32)
    for i in range(NCORES):
        r = res.results[i]
        b, par = i // 2, i % 2
        yo = r["yo"]
        y_prompt[4 * i:4 * i + 4] = yo[:NTOK_P].reshape(4, S, D)
        y_sample[b, par * 1024:(par + 1) * 1024] = yo[NTOK_P:]
        new_k[4 * i:4 * i + 4, 0] = r["nk"].reshape(4, S, 8, 64)
        new_v[4 * i:4 * i + 4, 0] = r["nv"].reshape(4, S, 8, 64)
    return (y_prompt, y_sample, new_k, new_v)
```

```python
import contextlib
import numpy as np
import concourse.bass as bass
import concourse.mybir as mybir
from concourse.bass_utils import run_bass_kernel_spmd

F32 = mybir.dt.float32
BF16 = mybir.dt.bfloat16
AF = mybir.ActivationFunctionType
ALU = mybir.AluOpType

D = 1024
DFF = 2816
NJ = DFF // 128
NCORES = 8
NTOK_P = 1024
NTOK_S = 1024
NHALO = 256
NOWN = NTOK_P + NTOK_S
NALL = NOWN + NHALO
EPS = 1e-6
NVM = 42
NCST = 2 + 64 + 64 + 2 * NVM
NSLOT = 8
SLOT_ELEMS = 2048

TT_ALL = [(0, 512, 0), (512, 512, 0), (1024, 512, 1), (1536, 512, 1), (2048, 256, 1)]
TT_OWN = TT_ALL[:4]


def qb_tiles():
    tl = []
    for b in range(8):
        if b <= 1:
            cs = [0, 1, 2, 3]
        elif b >= 6:
            cs = [4, 5, 6, 7]
        else:
            cs = list(range(b - 2, b + 3))
        ent = [("o", c, c - b) for c in cs]
        if b == 0:
            ent += [("h", 0, -2), ("h", 1, -1)]
        if b == 1:
            ent += [("h", 1, -2)]
        if b == 6:
            ent += [("h", 0, 2)]
        if b == 7:
            ent += [("h", 0, 1), ("h", 1, 2)]
        tl.append(ent)
    return tl


class Tok:
    __slots__ = ("sem", "val")

    def __init__(self, sem, val):
        self.sem = sem
        self.val = val


class Res:
    __slots__ = ("writer", "readers")

    def __init__(self):
        self.writer = None
        self.readers = []


class Prog:
    ENGS = ("pe", "act", "dve", "pool", "sp")

    def __init__(self, nc, es):
        self.nc = nc
        self.es = es
        self.sems = {}
        self.cnt = {}
        self.streams = {e: [] for e in self.ENGS}
        self.seen = {e: {} for e in self.ENGS}
        self.res = {}
        self.pending = {}
        for e in ("pe", "act", "dve", "pool"):
            self.sem(e)

    def barrier(self):
        toks = [Tok(e, self.cnt[e]) for e in ("pe", "act", "dve") if self.cnt[e] > 0]
        toks += [Tok(s, v) for s, v in self.cnt.items()
                 if s not in ("pe", "act", "dve", "pool") and not s.startswith("w") and v > 0]
        for e in ("pe", "act", "dve", "sp"):
            self.pending.setdefault(e, []).extend(toks)

    def sem(self, name):
        if name not in self.sems:
            self.sems[name] = self.es.enter_context(self.nc.semaphore("s_" + name))
            self.cnt[name] = 0
        return self.sems[name]

    def R(self, *key):
        r = self.res.get(key)
        if r is None:
            r = Res()
            self.res[key] = r
        return r

    def emit(self, eng, fn, reads=(), writes=(), deps=(), inc=True, dma_sem=None):
        toks = [t for t in deps if t is not None]
        toks.extend(self.pending.pop(eng, []))
        for r in reads:
            if r.writer is not None:
                toks.append(r.writer)
        for w in writes:
            if w.writer is not None:
                toks.append(w.writer)
            toks.extend(w.readers)
        need = {}
        for t in toks:
            if eng == "pe" and t.sem == "pe":
                continue
            if need.get(t.sem, 0) < t.val:
                need[t.sem] = t.val
        waits = []
        seen = self.seen[eng]
        for s, v in need.items():
            if seen.get(s, 0) >= v:
                continue
            seen[s] = v
            waits.append((s, v))
        if dma_sem is not None:
            self.sem(dma_sem)
            self.cnt[dma_sem] += 16
            tok = Tok(dma_sem, self.cnt[dma_sem])
            incspec = (dma_sem, 16)
        elif inc:
            self.cnt[eng] += 1
            tok = Tok(eng, self.cnt[eng])
            incspec = (eng, 1)
        else:
            tok = Tok(eng, self.cnt[eng] + 1)
            incspec = None
        self.streams[eng].append((waits, fn, incspec))
        for r in reads:
            r.readers.append(tok)
        for w in writes:
            w.writer = tok
            w.readers = []
        return tok

    def replay(self, eng, engine, final_waits=()):
        for waits, fn, incspec in self.streams[eng]:
            for s, v in waits:
                engine.wait_ge(self.sems[s], v)
            ins = fn(engine)
            if incspec is not None:
                ins.then_inc(self.sems[incspec[0]], incspec[1])
        for s, v in final_waits:
            engine.wait_ge(self.sems[s], v)


class Arena:
    def __init__(self, ap, nwords):
        self.ap = ap
        self.n = nwords
        self.off = 0

    def mark(self):
        return self.off

    def reset(self, m):
        self.off = m

    def take(self, nbytes, dtype, pattern=None, **dims):
        nw = (nbytes + 3) // 4
        nw = (nw + 7) // 8 * 8
        assert self.off + nw <= self.n, ("SBUF arena overflow", self.off, nw, self.n)
        v = self.ap[:, self.off:self.off + nw]
        self.off += nw
        if dtype == BF16:
            v = v.bitcast(BF16)[:, 0:nbytes // 2]
        else:
            v = v[:, 0:nbytes // 4]
        if pattern is not None:
            v = v.rearrange(pattern, **dims)
        return v


class PsumPool:
    def __init__(self, P, banks, name):
        self.P = P
        self.banks = banks
        self.i = 0
        self.name = name

    def next(self):
        b = self.banks[self.i % len(self.banks)]
        self.i += 1
        return b


def build_program(with_mixer=True, debug=False, stops=("merge", "merge")):
    LV = {None: 0, "q": 0.3, "k": 0.6, "qkv": 1, "att": 2, "pool": 3, "merge": 4}
    nc = bass.Bass("TRN2", target_bir_lowering=False)
    es = contextlib.ExitStack()

    def din(name, shape):
        return nc.dram_tensor(name, list(shape), F32, kind="ExternalInput").ap()

    def dout(name, shape):
        return nc.dram_tensor(name, list(shape), F32, kind="ExternalOutput").ap()

    xin = din("xin", [NALL, D])
    condT = din("condT", [128, 16])
    w_ada = din("w_ada", [D, 9 * D])
    b_adaT = din("b_adaT", [128, 72])
    gnT = din("gnT", [128, 24])
    w_ff_in = [din("w_ff1_in", [D, 2 * DFF]), din("w_ff2_in", [D, 2 * DFF])]
    w_ff_out = [din("w_ff1_out", [DFF, D]), din("w_ff2_out", [DFF, D])]
    ident_d = din("ident", [128, 128])
    w_in = din("w_in", [D, 4096])
    w_pool = din("w_pool", [4, 128, 128])
    w_br_pool = din("w_br_pool", [512, D])
    w_br_att = din("w_br_att", [512, D])
    w_out = din("w_out", [D, D])
    gainT = din("gainT", [128, 6])
    ck_d = din("ck", [512, 512])
    cv_d = din("cv", [512, 512])
    eb2_d = din("eb2src", [128, 8 * 14 * 64])
    cmask_d = din("colmask", [128, 64])
    cst_d = din("cst", [128, NCST])
    yo = dout("yo", [NOWN, D])
    nk_d = dout("nk", [NTOK_P, 512])
    nv_d = dout("nv", [NTOK_P, 512])

    with es:
        arena_t = es.enter_context(nc.sbuf_tensor("arena", [128, 51200], F32))
        psum_t = [es.enter_context(nc.psum_tensor("ps%d" % i, [128, 512], F32)) for i in range(8)]
        P = Prog(nc, es)
        A = Arena(arena_t[:, :], 51200)

        X = A.take(8 * NOWN * 4, F32, "p (c t) -> p c t", c=8)
        XN = A.take(8 * NALL * 2, BF16, "p (c t) -> p c t", c=8)
        WR = A.take(NSLOT * SLOT_ELEMS * 2, BF16, "p (s e) -> p s e", s=NSLOT)
        IDENT = A.take(128 * 4, F32)
        ONES = A.take(128 * 2, BF16)
        MODT = A.take(72 * 2 * 4, F32, "p (b k) -> p b k", k=2)
        BADA = A.take(72 * 4, F32)
        GN = A.take(24 * 4, F32, "p (n c) -> p n c", n=3)
        DER = A.take(9 * 8 * 2 * 4, F32, "p (i c k) -> p i c k", i=9, c=8)
        CONDF = A.take(16 * 4, F32)
        CONDB = A.take(16 * 2, BF16, "p (c k) -> p c k", k=2)
        scratch0 = A.mark()

        banks = [P.R("bank", i) for i in range(8)]
        poolA = PsumPool(P, [0, 1, 2, 3], "A")
        poolB = PsumPool(P, [4, 5], "B")
        poolC = PsumPool(P, [6, 7], "C")

        def bank_ap(b):
            return psum_t[b]

        wsched = []

        def plan(tag, kind, ap, a0, n):
            wsched.append((tag, kind, ap, a0, n))

        wstate = {"next_load": 0, "next_acq": 0, "released": set()}
        wslot_res = [P.R("wslot", s) for s in range(NSLOT)]

        def _wload(u):
            tag, kind, ap, a0, n = wsched[u]
            s = u % NSLOT
            if kind == "kxn":
                nkc = ap.shape[0] // 128
                src = ap[:, a0:a0 + n].rearrange("(kc p) n -> p kc n", p=128)
                dst = WR[:, s, 0:nkc * n].rearrange("p (kc n) -> p kc n", kc=nkc)
            else:
                src = ap[a0:a0 + n * 128, :].rearrange("(r p) n -> p r n", p=128)
                dst = WR[:, s, 0:n * 1024].rearrange("p (r n) -> p r n", r=n)
            P.emit("pool", lambda e, dst=dst, src=src: e.dma_start(out=dst, in_=src),
                   writes=[wslot_res[s]], dma_sem="w%d" % s)

        def _wpump():
            while wstate["next_load"] < len(wsched):
                u = wstate["next_load"]
                if u >= NSLOT and (u - NSLOT) not in wstate["released"]:
                    break
                _wload(u)
                wstate["next_load"] += 1

        def wacq(tag):
            u = wstate["next_acq"]
            wstate["next_acq"] += 1
            assert wsched[u][0] == tag, (wsched[u][0], tag)
            _wpump()
            assert wstate["next_load"] > u, "weight ring deadlock"
            _, kind, ap, a0, n = wsched[u]
            s = u % NSLOT
            if kind == "kxn":
                nkc = ap.shape[0] // 128
                view = WR[:, s, 0:nkc * n].rearrange("p (kc n) -> p kc n", kc=nkc)
            else:
                view = WR[:, s, 0:n * 1024].rearrange("p (r n) -> p r n", r=n)
            return u, view, wslot_res[s]

        def wrel(u):
            wstate["released"].add(u)
            _wpump()

        ff_blocks = [(j0, 2) for j0 in range(0, NJ, 2)]

        def plan_ada(u):
            plan(("ada", u), "kxn", w_ada, u * 256, 256)

        def plan_ff(which, bi):
            j0, nj = ff_blocks[bi]
            plan(("ffg", which, bi), "kxn", w_ff_in[which], j0 * 128, nj * 128)
            plan(("ffu", which, bi), "kxn", w_ff_in[which], DFF + j0 * 128, nj * 128)
            plan(("ffd", which, bi), "rows", w_ff_out[which], j0 * 128, nj)

        for u in range(8):
            plan_ada(u)
        ada_next = 12
        for bi in range(len(ff_blocks)):
            plan_ff(0, bi)
            if bi == 0:
                for u in range(8, 12):
                    plan_ada(u)
            for _ in range(3):
                if ada_next < 36:
                    plan_ada(ada_next)
                    ada_next += 1
        assert ada_next == 36
        if with_mixer:
            for half in range(2):
                lv = LV[stops[half]]
                for nm, c0 in (("q", 512), ("k", 1024), ("v", 1536), ("p", 0)):
                    if lv < {"q": 0.3, "k": 0.6, "v": 1, "p": 3}[nm]:
                        continue
                    for u in range(2):
                        plan(("win", half, nm, u), "kxn", w_in, c0 + u * 256, 256)
                if lv < 4:
                    continue
                for jp in range(4):
                    plan(("g0", half, jp), "kxn", w_in, 2048 + jp * 256, 256)
                    plan(("g1", half, jp), "kxn", w_in, 3072 + jp * 256, 256)
                    plan(("brp", half, jp), "kxn", w_br_pool, jp * 256, 256)
                    plan(("bra", half, jp), "kxn", w_br_att, jp * 256, 256)
                for mu in range(4):
                    plan(("wo", half, mu), "kxn", w_out, mu * 256, 256)
        for bi in range(len(ff_blocks)):
            plan_ff(1, bi)

        def mm(out, lhsT, rhs, start, stop, reads, writes, inc, sgc=False):
            if sgc:
                return P.emit("pe", lambda e: e.matmul(out, lhsT=lhsT, rhs=rhs, start=start, stop=stop,
                                                       skip_group_check=True),
                              reads=reads, writes=writes, inc=inc)
            return P.emit("pe", lambda e: e.matmul(out, lhsT=lhsT, rhs=rhs, start=start, stop=stop),
                          reads=reads, writes=writes, inc=inc)

        def tr(out, in_, reads, writes, inc):
            return P.emit("pe", lambda e: e.transpose(out=out, in_=in_, identity=IDENT),
                          reads=list(reads) + [r_const], writes=writes, inc=inc)

        def actv(out, in_, func, reads, writes, bias=None, scale=None):
            kw = {}
            if bias is not None:
                kw["bias"] = bias
            if scale is not None:
                kw["scale"] = scale
            return P.emit("act", lambda e: e.activation(out=out, in_=in_, func=func, **kw),
                          reads=reads, writes=writes)

        def vtt(out, in0, in1, op, reads, writes):
            return P.emit("dve", lambda e: e.tensor_tensor(out=out, in0=in0, in1=in1, op=op),
                          reads=reads, writes=writes)

        def vstt(out, in0, scalar, in1, op0, op1, reads, writes):
            return P.emit("dve", lambda e: e.scalar_tensor_tensor(out=out, in0=in0, scalar=scalar, in1=in1,
                                                                   op0=op0, op1=op1),
                          reads=reads, writes=writes)

        def vts(out, in0, s1, op0, reads, writes, s2=None, op1=None):
            if op1 is None:
                return P.emit("dve", lambda e: e.tensor_scalar(out=out, in0=in0, scalar1=s1, scalar2=None, op0=op0),
                              reads=reads, writes=writes)
            return P.emit("dve", lambda e: e.tensor_scalar(out=out, in0=in0, scalar1=s1, scalar2=s2,
                                                            op0=op0, op1=op1), reads=reads, writes=writes)

        def vcopy(out, in_, reads, writes):
            return P.emit("dve", lambda e: e.tensor_copy(out=out, in_=in_), reads=reads, writes=writes)

        def vrecip(out, in_, reads, writes):
            return P.emit("dve", lambda e: e.reciprocal(out=out, in_=in_), reads=reads, writes=writes)

        def vmemset(ap, val, writes):
            return P.emit("dve", lambda e: e.memset(ap, val), writes=writes)

        def sp_dma(out, in_, sem, reads=(), writes=()):
            return P.emit("sp", lambda e: e.dma_start(out=out, in_=in_), reads=reads, writes=writes, dma_sem=sem)

        r_const = P.R("const")
        sp_dma(IDENT, ident_d[:, :], "const", writes=[r_const])
        sp_dma(BADA, b_adaT[:, :], "const", writes=[r_const])
        sp_dma(GN.rearrange("p n c -> p (n c)"), gnT[:, :], "const", writes=[r_const])
        sp_dma(CONDF, condT[:, :], "const", writes=[r_const])
        r_ones = P.R("ones")
        vmemset(ONES, 1.0 / 1024.0, [r_ones])
        EPSC = A.take(4, F32)
        r_eps = P.R("epsc")
        vmemset(EPSC, EPS, [r_eps])
        r_condb = P.R("condb")
        actv(CONDB.rearrange("p c k -> p (c k)"), CONDF, AF.Silu, [r_const], [r_condb])

        WPOOL = A.take(4 * 128 * 2, BF16, "p (g d) -> p g d", g=4)
        ONESBLK = A.take(128 * 2, BF16)
        ONESB = A.take(64 * 2, BF16)
        GAINS = A.take(6 * 4, F32)
        QG = A.take(4, F32)
        CST = A.take(NCST * 4, F32)
        CMASK = A.take(64 * 4, F32)
        r_mc = P.R("mixconst")
        r_wpool = P.R("wpool")
        if with_mixer:
            sp_dma(GAINS, gainT[:, :], "const", writes=[r_const])
            sp_dma(CST, cst_d[:, :], "const", writes=[r_const])
            sp_dma(CMASK, cmask_d[:, :], "const", writes=[r_const])
            P.emit("pool", lambda e: e.dma_start(out=WPOOL, in_=w_pool.rearrange("g c d -> c g d")),
                   writes=[r_wpool], dma_sem="wpool")
            vmemset(ONESBLK, 0.0, [r_mc])
            vmemset(ONESBLK[0:64, 0:64], 1.0 / 64.0, [r_mc])
            vmemset(ONESBLK[64:128, 64:128], 1.0 / 64.0, [r_mc])
            vmemset(ONESB, 1.0, [r_mc])
            vts(QG, GAINS[:, 0:1], 0.125, ALU.mult, [r_const], [r_mc])
        KG = GAINS[:, 1:2]
        PSC = GAINS[:, 2:6]
        FLG = CST[:, 0:2]
        EDGE_P = CST[:, 2:66].rearrange("p (g e) -> p g e", g=4)
        EDGE_S = CST[:, 66:130].rearrange("p (g e) -> p g e", g=4)
        VM = CST[:, 130:130 + 2 * NVM]

        mix0 = A.mark()
        XH = A.take(8 * NHALO * 4, F32, "p (c t) -> p c t", c=8)
        RSTD = A.take(NALL * 4, F32)
        TMPA = A.take(2 * 512 * 4, F32, "p (s t) -> p s t", s=2)
        TMPB = A.take(2 * 512 * 4, F32, "p (s t) -> p s t", s=2)
        HB = A.take(2 * 2 * 512 * 2, BF16, "p (s j t) -> p s j t", s=2, j=2)
        SIL = A.take(2 * 512 * 4, F32, "p (s t) -> p s t", s=2)
        STG = A.take(4 * 1024 * 4, F32, "p (s t) -> p s t", s=4)
        ffend = A.mark()

        def xview(c, t0, n):
            if t0 < NOWN:
                return X[:, c, t0:t0 + n]
            return XH[:, c, t0 - NOWN:t0 - NOWN + n]

        def rx(c, t0):
            return P.R("x", c, t0 // 512)

        def rxn(c, t0):
            return P.R("xn", c, t0 // 512)

        stg_res = [[P.R("stg", s, h) for h in range(2)] for s in range(4)]

        def load_x_tile(tt):
            t00, nn, _c = TT_ALL[tt]
            for i in range(t00 // 128, (t00 + nn) // 128):
                load_x_sub(i)

        def load_x_sub(i):
            s = i % 4
            sp_dma(STG[:, s, :], xin[i * 128:(i + 1) * 128, :], "stg%d" % s, writes=stg_res[s])
            t0 = i * 128
            for half in range(2):
                b = poolC.next()
                for cc in range(4):
                    c = half * 4 + cc
                    tr(bank_ap(b)[:, cc * 128:(cc + 1) * 128], STG[:, s, c * 128:(c + 1) * 128],
                       [stg_res[s][half]], [banks[b]], inc=(cc == 3))
                if t0 < NOWN:
                    dst = X[:, half * 4:half * 4 + 4, t0:t0 + 128]
                else:
                    dst = XH[:, half * 4:half * 4 + 4, t0 - NOWN:t0 - NOWN + 128]
                wr = [rx(half * 4 + cc, t0) for cc in range(4)]
                srcv = bank_ap(b)[:, :].rearrange("p (c t) -> p c t", c=4)
                if half == 0:
                    actv(dst, srcv, AF.Copy, [banks[b]], wr)
                else:
                    vcopy(dst, srcv, [banks[b]], wr)

        load_x_tile(0)
        load_x_tile(1)

        r_mod = P.R("modT")
        r_der = [P.R("der", i) for i in range(9)]

        def ada_units(us):
            b = poolC.next()
            col = 0
            blks = []
            for u in us:
                uu, view, wres = wacq(("ada", u))
                for bb in range(2):
                    blk = 2 * u + bb
                    for kc in range(8):
                        mm(bank_ap(b)[:, col:col + 2], view[:, kc, bb * 128:(bb + 1) * 128], CONDB[:, kc, :],
                           kc == 0, kc == 7, [wres, r_condb], [banks[b]], inc=(kc == 7))
                    blks.append((blk, col))
                    col += 2
                wrel(uu)
            blk0 = blks[0][0]
            nb = len(blks)
            for k in range(2):
                vtt(MODT[:, blk0:blk0 + nb, k], bank_ap(b)[:, 0:2 * nb].rearrange("p (b k) -> p b k", k=2)[:, :, k],
                    BADA[:, blk0:blk0 + nb], ALU.add, [banks[b], r_const], [r_mod])

        def derive(i):
            kind = i % 3
            n = i // 3
            srcv = MODT[:, i * 8:(i + 1) * 8, :]
            if kind == 0:
                vcopy(DER[:, i, :, :], srcv, [r_mod], [r_der[i]])
            elif kind == 1:
                for k in range(2):
                    vstt(DER[:, i, :, k], MODT[:, i * 8:(i + 1) * 8, k], 1.0, GN[:, n, :], ALU.add, ALU.mult,
                         [r_mod, r_const], [r_der[i]])
            else:
                vts(DER[:, i, :, :], srcv, 1.0 if n == 1 else 0.5, ALU.mult, [r_mod], [r_der[i]])

        ada_units(list(range(0, 8)))
        for i in range(2):
            derive(i)

        def ptt(out, in0, in1, op, reads, writes):
            return P.emit("pool", lambda e: e.tensor_tensor(out=out, in0=in0, in1=in1, op=op),
                          reads=reads, writes=writes)

        def pts(out, in0, s1, reads, writes):
            return P.emit("pool", lambda e: e.tensor_scalar(out=out, in0=in0, scalar1=s1, scalar2=None, op0=ALU.mult),
                          reads=reads, writes=writes)

        norm_cnt = [0]

        def norm_tile(nidx, tile, tmp=None, part="ab", pool_ok=True):
            ib = nidx * 3
            ia = nidx * 3 + 1
            t0, n, cond = tile
            if tmp is None:
                TMPA_, TMPB_, RS_, tag = TMPA, TMPB, RSTD[:, t0:t0 + n], "f"
            else:
                TMPA_, TMPB_, RSl, tag = tmp
                RS_ = RSl[:, (t0 // 512) % 2, 0:n]
            if part in ("a", "ab"):
                for c in range(8):
                    if c % 2 == 0 and pool_ok:
                        ptt(XN[:, c, t0:t0 + n], xview(c, t0, n), xview(c, t0, n), ALU.mult, [rx(c, t0)], [rxn(c, t0)])
                    else:
                        actv(XN[:, c, t0:t0 + n], xview(c, t0, n), AF.Square, [rx(c, t0)], [rxn(c, t0)])
            if part == "a":
                return
            b = poolB.next()
            for c in range(8):
                mm(bank_ap(b)[:, 0:n], ONES, XN[:, c, t0:t0 + n], c == 0, c == 7,
                   [rxn(c, t0), r_ones], [banks[b]], inc=(c == 7))
            s = norm_cnt[0] % 2
            norm_cnt[0] += 1
            rt = P.R("tmpa", tag, s)
            actv(TMPA_[:, s, 0:n], bank_ap(b)[:, 0:n], AF.Ln, [banks[b], r_eps], [rt], bias=EPSC, scale=1.0)
            rr = P.R("rstd", tag, t0 // 512)
            actv(RS_, TMPA_[:, s, 0:n], AF.Exp, [rt], [rr], scale=-0.5)
            for c in range(8):
                s2 = c % 2
                rb = P.R("tmpb", tag, s2)
                vstt(TMPB_[:, s2, 0:n], xview(c, t0, n), DER[:, ia, c, cond:cond + 1], RS_,
                     ALU.mult, ALU.mult, [rx(c, t0), rr, r_der[ia]], [rb])
                actv(XN[:, c, t0:t0 + n], TMPB_[:, s2, 0:n], AF.Identity, [rb, r_der[ib]], [rxn(c, t0)],
                     bias=DER[:, ib, c, cond:cond + 1], scale=1.0)

        def norm_phase(nidx, tiles):
            for tile in tiles:
                norm_tile(nidx, tile)

        hb_res = [[P.R("hb", s, j) for j in range(2)] for s in range(2)]
        sil_res = [P.R("sil", s) for s in range(2)]
        silc = [0]

        def ff_GU(G, U, rg, ru, nj, tile, hs, jjs=None):
            t0, n, cond = tile
            for jj in (range(nj) if jjs is None else jjs):
                ba = poolA.next()
                bu = poolA.next()
                for kc in range(8):
                    mm(bank_ap(ba)[:, 0:n], G[:, kc, jj * 128:(jj + 1) * 128], XN[:, kc, t0:t0 + n],
                       kc == 0, kc == 7, [rg, rxn(kc, t0)], [banks[ba]], inc=(kc == 7))
                for kc in range(8):
                    mm(bank_ap(bu)[:, 0:n], U[:, kc, jj * 128:(jj + 1) * 128], XN[:, kc, t0:t0 + n],
                       kc == 0, kc == 7, [ru, rxn(kc, t0)], [banks[bu]], inc=(kc == 7))
                ss = silc[0] % 2
                silc[0] += 1
                actv(SIL[:, ss, 0:n], bank_ap(ba)[:, 0:n], AF.Silu, [banks[ba]], [sil_res[ss]])
                vtt(HB[:, hs, jj, 0:n], SIL[:, ss, 0:n], bank_ap(bu)[:, 0:n], ALU.mult,
                    [sil_res[ss], banks[bu]], [hb_res[hs][jj]])

        def ff_DN(Dn, rd, nj, tile, hs, ig, ms=range(8)):
            t0, n, cond = tile
            for m in ms:
                by = poolDN.next()
                for jj in range(nj):
                    mm(bank_ap(by)[:, 0:n], Dn[:, jj, m * 128:(m + 1) * 128], HB[:, hs, jj, 0:n],
                       jj == 0, jj == nj - 1, [rd, hb_res[hs][jj]], [banks[by]], inc=(jj == nj - 1))
                vstt(xview(m, t0, n), bank_ap(by)[:, 0:n], DER[:, ig, m, cond:cond + 1], xview(m, t0, n),
                     ALU.mult, ALU.add, [banks[by], r_der[ig]], [rx(m, t0)])

        poolDN = PsumPool(P, [4, 5, 6, 7], "DN")

        def ff_phase(which, nidx, tiles, after_block=None, tile_done_a=None, tile_done_b=None, norm_done=(),
                     lazy_load=None, blk0_hook=None, defer_last=False):
            ig = nidx * 3 + 2
            ntt = len(tiles)
            for ti in range(ntt):
                if ti not in norm_done and (lazy_load is None or ti < 2):
                    norm_tile(nidx, tiles[ti], part="a", pool_ok=not (which == 0 and ti == 0))
            if 0 not in norm_done:
                norm_tile(nidx, tiles[0], part="b")
            nb = len(ff_blocks)
            for bi, (j0, nj) in enumerate(ff_blocks):
                ug, G, rg = wacq(("ffg", which, bi))
                uu, U, ru = wacq(("ffu", which, bi))
                ud, Dn, rd = wacq(("ffd", which, bi))
                late = []
                for step in range(ntt + 1):
                    if bi == 0 and lazy_load is not None and step + 2 < ntt:
                        lazy_load(step + 2)
                        norm_tile(nidx, tiles[step + 2], part="a")
                    if bi == 0 and step + 1 < ntt and (step + 1) not in norm_done:
                        norm_tile(nidx, tiles[step + 1], part="b")
                    if step < ntt:
                        ff_GU(G, U, rg, ru, nj, tiles[step], step % 2, jjs=[0])
                    if step >= 1:
                        ff_DN(Dn, rd, nj, tiles[step - 1], (step - 1) % 2, ig, ms=range(0, 4))
                    if step < ntt:
                        ff_GU(G, U, rg, ru, nj, tiles[step], step % 2, jjs=range(1, nj))
                    if bi == 0 and step == 0 and blk0_hook is not None:
                        blk0_hook()
                    if step == ntt:
                        wrel(ug)
                        wrel(uu)
                    if bi == nb - 1 and tile_done_b is not None:
                        for ti in late:
                            tile_done_b(ti)
                        late = []
                    if step >= 1:
                        ff_DN(Dn, rd, nj, tiles[step - 1], (step - 1) % 2, ig, ms=range(4, 8))
                        if bi == nb - 1:
                            if tile_done_a is not None:
                                tile_done_a(step - 1)
                            late.append(step - 1)
                wrel(ud)
                leftover = []
                if bi == nb - 1 and tile_done_b is not None:
                    if defer_last:
                        leftover = list(late)
                    else:
                        for ti in late:
                            tile_done_b(ti)
                if after_block is not None:
                    after_block(bi)
            return leftover

        ocnt = [0]
        out_toks = []

        def out_tile(ti):
            t0, n, cond = TT_OWN[ti]
            for sub in range(n // 128):
                tk = t0 + sub * 128
                s = ocnt[0] % 4
                ocnt[0] += 1
                for half in range(2):
                    b = poolC.next()
                    for cc in range(4):
                        c = half * 4 + cc
                        tr(bank_ap(b)[:, cc * 128:(cc + 1) * 128], X[:, c, tk:tk + 128],
                           [rx(c, tk)], [banks[b]], inc=(cc == 3))
                    if half == 0:
                        actv(STG[:, s, 0:512], bank_ap(b)[:, :], AF.Copy, [banks[b]], [stg_res[s][0]])
                    else:
                        vcopy(STG[:, s, 512:1024], bank_ap(b)[:, :], [banks[b]], [stg_res[s][1]])
                tok = sp_dma(yo[tk:tk + 128, :], STG[:, s, :], "stg%d" % s, reads=stg_res[s])
                out_toks.append(tok)


        def mixer_half(half, q_hooks=()):
            is_s = (half == 1)
            q_hooks = list(q_hooks)
            own = TT_ALL[2:4] if is_s else TT_ALL[0:2]
            kvt = own + ([TT_ALL[4]] if is_s else [])
            base = 1024 if is_s else 0
            nkv = 1280 if is_s else 1024
            cond = 1 if is_s else 0

            def lc(t0):
                return t0 - base

            if is_s:
                P.barrier()
            A.reset(mix0)
            ATT = A.take(4 * 1024 * 2, BF16, "p (h t) -> p h t", h=4)
            m1 = A.mark()
            KT = A.take(4 * nkv * 2, BF16, "p (h t) -> p h t", h=4)
            VB = A.take((nkv // 128) * 512 * 2, BF16, "p (i f) -> p i f", f=512)
            if is_s:
                KCT = A.take(4 * 512 * 2, BF16, "p (h t) -> p h t", h=4)
                VC = A.take(4 * 512 * 2, BF16, "p (i f) -> p i f", f=512)
            else:
                A.take(1024, F32)
                assert (A.mark() - mix0) * 4 >= 25600
            QT = A.take(4 * 1024 * 2, BF16, "p (h t) -> p h t", h=4)
            m2 = A.mark()
            SQ = A.take(2 * 512 * 2, BF16, "p (s t) -> p s t", s=2)
            TA1 = A.take(2 * 512 * 4, F32, "p (s t) -> p s t", s=2)
            TB1 = A.take(2 * 512 * 4, F32, "p (s t) -> p s t", s=2)
            if is_s:
                STGC = A.take(4 * 512 * 4, F32, "p (s t) -> p s t", s=4)
            else:
                KNF2 = A.take(2 * 4 * 512 * 4, F32, "p (s h t) -> p s h t", s=2, h=4)
                OST = A.take(2 * 512 * 4, F32, "p (s t) -> p s t", s=2)

            r_qt = [P.R("qt", half, hp) for hp in range(4)]
            r_kt = [P.R("kt", half, hp) for hp in range(4)]
            r_vb = [P.R("vb", half, i) for i in range(nkv // 128)]
            r_att = [P.R("att", half, hp) for hp in range(4)]
            r_sq = [P.R("sq", s) for s in range(2)]
            r_ta1 = [P.R("ta1", s) for s in range(2)]
            r_tb1 = [P.R("tb1", s) for s in range(2)]
            cnt = {"n": 0, "q": 0}

            def qk_proj(nm, tiles, dst, r_dst, gain_col):
                us = [wacq(("win", half, nm, u)) for u in range(2)]
                pend = []
                want_nk = (nm == "k" and not is_s)

                def finish(item):
                    b, s, hp, t0, n, slot = item
                    b2 = poolB.next()
                    mm(bank_ap(b2)[:, 0:n], ONESBLK, SQ[:, s, 0:n], True, True, [r_sq[s], r_mc], [banks[b2]], inc=True)
                    actv(TA1[:, s, 0:n], bank_ap(b2)[:, 0:n], AF.Ln, [banks[b2], r_eps], [r_ta1[s]],
                         bias=EPSC, scale=1.0)
                    actv(TB1[:, s, 0:n], TA1[:, s, 0:n], AF.Exp, [r_ta1[s]], [r_tb1[s]], scale=-0.5)
                    if want_nk:
                        rk = P.R("knf", slot, hp)
                        vstt(KNF2[:, slot, hp, 0:n], bank_ap(b)[:, 0:n], gain_col, TB1[:, s, 0:n], ALU.mult, ALU.mult,
                             [banks[b], r_tb1[s], r_mc, r_const], [rk])
                        vcopy(dst[:, hp, lc(t0):lc(t0) + n], KNF2[:, slot, hp, 0:n], [rk], [r_dst[hp]])
                    else:
                        vstt(dst[:, hp, lc(t0):lc(t0) + n], bank_ap(b)[:, 0:n], gain_col, TB1[:, s, 0:n],
                             ALU.mult, ALU.mult, [banks[b], r_tb1[s], r_mc, r_const], [r_dst[hp]])

                def emit_nk(t0, slot):
                    for sub in range(4):
                        b3 = poolC.next()
                        for hp in range(4):
                            tr(bank_ap(b3)[:, hp * 128:(hp + 1) * 128], KNF2[:, slot, hp, sub * 128:(sub + 1) * 128],
                               [P.R("knf", slot, hp)], [banks[b3]], inc=(hp == 3))
                        so = cnt["n"] % 2
                        cnt["n"] += 1
                        ro = P.R("ost", so)
                        vcopy(OST[:, so, :], bank_ap(b3)[:, :], [banks[b3]], [ro])
                        tk = t0 + sub * 128
                        out_toks.append(sp_dma(nk_d[tk:tk + 128, :], OST[:, so, :], "ost%d" % so, reads=[ro]))

                nk_pending = []
                for ti_, (t0, n, _c) in enumerate(tiles):
                    for u in range(2):
                        uu, W, rw = us[u]
                        for bb in range(2):
                            hp = 2 * u + bb
                            b = poolA.next()
                            for kc in range(8):
                                mm(bank_ap(b)[:, 0:n], W[:, kc, bb * 128:(bb + 1) * 128], XN[:, kc, t0:t0 + n],
                                   kc == 0, kc == 7, [rw, rxn(kc, t0)], [banks[b]], inc=(kc == 7))
                            s = cnt["q"] % 2
                            cnt["q"] += 1
                            actv(SQ[:, s, 0:n], bank_ap(b)[:, 0:n], AF.Square, [banks[b]], [r_sq[s]])
                            pend.append((b, s, hp, t0, n, ti_ % 2))
                            if len(pend) > 1:
                                finish(pend.pop(0))
                    if nm == "q" and q_hooks:
                        q_hooks.pop(0)()
                    if want_nk:
                        for (pt0, pslot) in nk_pending:
                            emit_nk(pt0, pslot)
                        nk_pending = [(t0, ti_ % 2)]
                while pend:
                    finish(pend.pop(0))
                tail = [(lambda a=pt0, b_=pslot: emit_nk(a, b_)) for (pt0, pslot) in nk_pending]
                for uu, W, rw in us:
                    wrel(uu)
                return tail

            qk_proj("q", own, QT, r_qt, QG)
            while q_hooks:
                q_hooks.pop(0)()
            if is_s:
                r_kct = P.R("kct")
                r_vc = P.R("vc")
                P.emit("pool", lambda e: e.dma_start(out=VC, in_=cv_d.rearrange("(i p) f -> p i f", p=128)),
                       writes=[r_vc], dma_sem="vcl")
                r_stgc = [P.R("stgc", s) for s in range(4)]
                for i in range(4):
                    s = i
                    sp_dma(STGC[:, s, :], ck_d[i * 128:(i + 1) * 128, :], "stgc%d" % s, writes=[r_stgc[s]])
                    b = poolC.next()
                    for hp in range(4):
                        tr(bank_ap(b)[:, hp * 128:(hp + 1) * 128], STGC[:, s, hp * 128:(hp + 1) * 128],
                           [r_stgc[s]], [banks[b]], inc=(hp == 3))
                    actv(KCT[:, :, i * 128:(i + 1) * 128], bank_ap(b)[:, :].rearrange("p (h t) -> p h t", h=4),
                         AF.Copy, [banks[b]], [r_kct])

            if LV[stops[half]] < 0.6:
                return
            if not is_s:
                P.barrier()
            nk_tail = qk_proj("k", kvt, KT, r_kt, KG)
            if LV[stops[half]] < 1:
                for f_ in nk_tail:
                    f_()
                return

            P.barrier()
            A.reset(m2)
            NT = 10
            PRW = A.take(2 * NT * 128 * 2, BF16, "p (s t) -> p s t", s=2)
            RC = A.take(2 * (128 if is_s else 256) * 4, F32, "p (s t) -> p s t", s=2)
            r_prw = [[[P.R("prw", s, k, q) for q in range(2)] for k in range(NT)] for s in range(2)]
            r_rc = [P.R("rc", s) for s in range(2)]
            poolS = PsumPool(P, [0, 1, 2, 3, 4, 5], "S")
            poolAcc = PsumPool(P, [6, 7], "Acc")
            if not is_s:
                VST = A.take(2 * 512 * 4, F32, "p (s t) -> p s t", s=2)
            if is_s:
                EBST = A.take(2 * 14 * 64 * 4, F32)
                EBH = A.take(2 * 14 * 64 * 2, BF16, "p (h s c) -> p h s c", h=2, s=14)
                EBI = A.take(2 * 10 * 64 * 2, BF16, "p (h a two c) -> p h a two c", h=2, a=5, two=2)
                r_ebst = P.R("ebst")
                r_ebh = P.R("ebh")
                r_ebi = P.R("ebi")
                tiles = qb_tiles()
                NQ = 128
            else:
                NQ = 256

            def eb_load(hp):
                sp_dma(EBST, eb2_d[:, hp * 1792:(hp + 1) * 1792], "ebst", writes=[r_ebst])
                actv(EBST, EBST, AF.Exp, [r_ebst], [r_ebst])

            def eb_build_h(a0=0, a1=28):
                for a in range(a0, a1):
                    vtt(EBH[:, a // 14, a % 14, :], EBST[:, a * 64:(a + 1) * 64], CMASK, ALU.mult,
                        [r_ebst, r_const], [r_ebh])

            def eb_build_i():
                ebi_flat = EBI.rearrange("p h a two c -> p h (a two c)")
                vcopy(ebi_flat, EBH[:, :, 2:12, :].rearrange("p h s c -> p h (s c)"), [r_ebh], [r_ebi])
                vmemset(ebi_flat[0:64, :, 0:64], 0.0, [r_ebi])
                vmemset(ebi_flat[64:128, :, 8 * 64:9 * 64], 0.0, [r_ebi])
                vmemset(ebi_flat[:, :, 9 * 64:10 * 64], 0.0, [r_ebi])

            if is_s:
                eb_load(0)
                eb_build_h()
                eb_build_i()
            us = [wacq(("win", half, "v", u)) for u in range(2)]
            for i in range(nkv // 128):
                tk = base + i * 128
                sv = i % 2
                rv = P.R("vst", sv)
                for u in range(2):
                    uu, W, rw = us[u]
                    b = poolA.next()
                    for kc in range(8):
                        mm(bank_ap(b)[:, 0:256], XN[:, kc, tk:tk + 128], W[:, kc, :], kc == 0, kc == 7,
                           [rw, rxn(kc, tk)], [banks[b]], inc=(kc == 7))
                    if is_s:
                        actv(VB[:, i, u * 256:(u + 1) * 256], bank_ap(b)[:, 0:256], AF.Copy, [banks[b]], [r_vb[i]])
                    else:
                        rvu = P.R("vst", sv, u)
                        actv(VST[:, sv, u * 256:(u + 1) * 256], bank_ap(b)[:, 0:256], AF.Copy, [banks[b]], [rvu])
                        vcopy(VB[:, i, u * 256:(u + 1) * 256], VST[:, sv, u * 256:(u + 1) * 256], [rvu], [r_vb[i]])
                if not is_s:
                    out_toks.append(sp_dma(nv_d[tk:tk + 128, :], VST[:, sv, :], "vst%d" % sv,
                                           reads=[P.R("vst", sv, 0), P.R("vst", sv, 1)]))
            for uu, W, rw in us:
                wrel(uu)
            for f_ in nk_tail:
                f_()

            if LV[stops[half]] < 2:
                return
            def unit_list(hp):
                ul = []
                if not is_s:
                    for sq in range(4):
                        tq = sq * 256
                        kl = []
                        for kc in range(2):
                            vt = 2 * sq + kc
                            kl.append((KT, tq + kc * 128, r_kt[hp], VB, vt, r_vb[vt], None))
                        ul.append((tq, kl))
                else:
                    vmi = 0
                    for b in range(8):
                        kl = [(KCT, i * 128, r_kct, VC, i, r_vc, None) for i in range(4)]
                        for ti, (kind, idx, delta) in enumerate(tiles[b]):
                            kcol = idx * 128 if kind == "o" else 1024 + idx * 128
                            vt = idx if kind == "o" else 8 + idx
                            kl.append((KT, kcol, r_kt[hp], VB, vt, r_vb[vt], (delta, vmi + ti)))
                        vmi += len(tiles[b])
                        ul.append((b * 128, kl))
                    assert vmi == NVM
                return ul

            ucount = {"n": 0, "r": 0}

            def emit_S(hp, h, tq, kl):
                pr = slice(h * 64, (h + 1) * 64)
                interior = is_s and (2 <= tq // 128 <= 5)
                if interior:
                    assert len(kl) == 9 and [m[6][0] for m in kl[4:]] == [-2, -1, 0, 1, 2]
                ps = ucount["n"] % 2
                ucount["n"] += 1
                per_bank = 512 // NQ
                for g0 in range(0, len(kl), per_bank):
                    grp = kl[g0:g0 + per_bank]
                    sb = poolS.next()
                    for k, (Ksrc, kcol, rk, _V, _vt, _rv, _m) in enumerate(grp):
                        mm(bank_ap(sb)[:, k * NQ:(k + 1) * NQ], Ksrc[pr, hp, kcol:kcol + 128], QT[pr, hp, tq:tq + NQ],
                           True, True, [rk, r_qt[hp]], [banks[sb]], inc=(k == len(grp) - 1))
                    wr = []
                    for k in range(len(grp)):
                        wr += r_prw[ps][g0 + k]
                    actv(PRW[:, ps, g0 * NQ:(g0 + len(grp)) * NQ], bank_ap(sb)[:, 0:len(grp) * NQ], AF.Exp,
                         [banks[sb]], wr)
                    for k, (_K, _kc, _rk, _V, _vt, _rv, minfo) in enumerate(grp):
                        if minfo is None or interior:
                            continue
                        delta, vi = minfo
                        for jq in range(2):
                            sidx = 2 * delta + 7 - jq
                            assert 0 <= sidx < 14
                            c0 = (g0 + k) * 128 + jq * 64
                            sl = PRW[:, ps, c0:c0 + 64]
                            vstt(sl, sl, VM[:, 2 * vi + jq:2 * vi + jq + 1], EBH[:, h, sidx, :],
                                 ALU.mult, ALU.mult, [r_ebh, r_const], [r_prw[ps][g0 + k][jq]])
                if interior:
                    pv = PRW[:, ps, :].rearrange("p (k q c) -> p k q c", k=NT, q=2)
                    for jq in range(2):
                        wr = [r_prw[ps][4 + k][jq] for k in range(5)]
                        vtt(pv[:, 4:9, jq, :], pv[:, 4:9, jq, :], EBI[:, h, :, 1 - jq, :], ALU.mult, [r_ebi], wr)
                return ps

            def emit_PV(hp, h, tq, kl, ps, acc):
                pr = slice(h * 64, (h + 1) * 64)
                nk = len(kl)
                for k, (_K, _kc, _rk, Vsrc, vt, rv, _m) in enumerate(kl):
                    rd = r_prw[ps][k] if NQ == 128 else (r_prw[ps][2 * k] + r_prw[ps][2 * k + 1])
                    mm(bank_ap(acc)[pr, 0:NQ], Vsrc[:, vt, hp * 128 + h * 64:hp * 128 + (h + 1) * 64],
                       PRW[:, ps, k * NQ:(k + 1) * NQ], k == 0, False, list(rd) + [rv], [banks[acc]], inc=False, sgc=True)
                    mm(bank_ap(acc)[pr, NQ:2 * NQ], ONESB, PRW[:, ps, k * NQ:(k + 1) * NQ], False, k == nk - 1,
                       list(rd) + [r_mc], [banks[acc]], inc=(k == nk - 1), sgc=True)

            def emit_norm(hp, tq, acc):
                rs = ucount["r"] % 2
                ucount["r"] += 1
                actv(RC[:, rs, 0:NQ], bank_ap(acc)[:, NQ:2 * NQ], AF.Ln, [banks[acc]], [r_rc[rs]])
                actv(RC[:, rs, 0:NQ], RC[:, rs, 0:NQ], AF.Exp, [r_rc[rs]], [r_rc[rs]], scale=-1.0)
                vtt(ATT[:, hp, tq:tq + NQ], bank_ap(acc)[:, 0:NQ], RC[:, rs, 0:NQ], ALU.mult,
                    [banks[acc], r_rc[rs]], [r_att[hp]])

            for hp in range(4):
                ul = unit_list(hp)
                if is_s:
                    ul = [ul[b] for b in (0, 1, 6, 7, 2, 3, 4, 5)]
                    if hp < 3:
                        eb_load(hp + 1)
                units = []
                for (tq, kl) in ul:
                    for h in range(2):
                        units.append((h, tq, kl))
                pend = None
                accs = {}
                for ui, (h, tq, kl) in enumerate(units):
                    ps = emit_S(hp, h, tq, kl)
                    if is_s and hp < 3 and 7 <= ui < 14:
                        eb_build_h(4 * (ui - 7), 4 * (ui - 6))
                    if pend is not None:
                        ph, ptq, pkl, pps = pend
                        if ph == 0:
                            accs[ptq] = poolAcc.next()
                        emit_PV(hp, ph, ptq, pkl, pps, accs[ptq])
                        if ph == 1:
                            emit_norm(hp, ptq, accs[ptq])
                    pend = (h, tq, kl, ps)
                if is_s and hp < 3:
                    eb_build_i()
                ph, ptq, pkl, pps = pend
                emit_PV(hp, ph, ptq, pkl, pps, accs[ptq])
                emit_norm(hp, ptq, accs[ptq])

            if LV[stops[half]] < 3:
                return
            P.barrier()
            A.reset(m1)
            YT = A.take(4 * 1024 * 2, BF16, "p (g t) -> p g t", g=4)
            m3 = A.mark()
            NS, L = (1, 1024) if is_s else (4, 256)
            LP = L + 16
            PP = A.take(4 * NS * LP * 4, F32, "p (g s l) -> p g s l", g=4, s=NS)
            TA = A.take(NS * LP * 4, F32, "p (s l) -> p s l", s=NS)
            TB = A.take(NS * LP * 4, F32, "p (s l) -> p s l", s=NS)
            TA2, TB2 = TA, TB
            DT = A.take(4 * 1024 * 2, BF16, "p (g t) -> p g t", g=4)
            r_pp = [P.R("pp", g) for g in range(4)]
            r_ta, r_tb = P.R("ta"), P.R("tb")
            r_dt = [P.R("dt", g) for g in range(4)]
            r_yt = [P.R("yt", g) for g in range(4)]
            for g in range(4):
                vmemset(PP[:, g, :, 0:8], 0.0, [r_pp[g]])
                vmemset(PP[:, g, :, L + 8:L + 16], 0.0, [r_pp[g]])
            us = [wacq(("win", half, "p", u)) for u in range(2)]
            for u in range(2):
                uu, W, rw = us[u]
                for bb in range(2):
                    g = 2 * u + bb
                    for ti, (t0, n, _c) in enumerate(own):
                        b = poolA.next()
                        for kc in range(8):
                            mm(bank_ap(b)[:, 0:n], W[:, kc, bb * 128:(bb + 1) * 128], XN[:, kc, t0:t0 + n],
                               kc == 0, kc == 7, [rw, rxn(kc, t0)], [banks[b]], inc=(kc == 7))
                        if is_s:
                            actv(PP[:, g, 0, 8 + ti * 512:8 + (ti + 1) * 512], bank_ap(b)[:, 0:512], AF.Copy,
                                 [banks[b]], [r_pp[g]])
                        else:
                            actv(PP[:, g, 2 * ti:2 * ti + 2, 8:8 + 256],
                                 bank_ap(b)[:, 0:512].rearrange("p (s l) -> p s l", s=2), AF.Copy, [banks[b]], [r_pp[g]])
                    if is_s:
                        t0, n, _c = TT_ALL[4]
                        b = poolA.next()
                        for kc in range(8):
                            mm(bank_ap(b)[:, 0:n], W[:, kc, bb * 128:(bb + 1) * 128], XN[:, kc, t0:t0 + n],
                               kc == 0, kc == 7, [rw, rxn(kc, t0)], [banks[b]], inc=(kc == 7))
                        actv(PP[:, g, 0, 0:8], bank_ap(b)[:, 248:256], AF.Identity, [banks[b], r_const], [r_pp[g]],
                             scale=FLG[:, 0:1])
                        actv(PP[:, g, 0, L + 8:L + 16], bank_ap(b)[:, 0:8], AF.Identity, [banks[b], r_const], [r_pp[g]],
                             scale=FLG[:, 1:2])
            for uu, W, rw in us:
                wrel(uu)
            pre_gates = None
            if LV[stops[half]] >= 4:
                pre_u0 = wacq(("g0", half, 0))
                pre_u1 = wacq(("g1", half, 0))
                pre_gates = []
                for ti, (t0, n, _c) in enumerate(own):
                    bg0, bg1 = 4 + 2 * ti, 5 + 2 * ti
                    for kc in range(8):
                        mm(bank_ap(bg0)[:, :], pre_u0[1][:, kc, 0:128], XN[:, kc, t0:t0 + 512], kc == 0, kc == 7,
                           [pre_u0[2], rxn(kc, t0)], [banks[bg0]], inc=(kc == 7))
                    for kc in range(8):
                        mm(bank_ap(bg1)[:, :], pre_u1[1][:, kc, 0:128], XN[:, kc, t0:t0 + 512], kc == 0, kc == 7,
                           [pre_u1[2], rxn(kc, t0)], [banks[bg1]], inc=(kc == 7))
                    pre_gates.append((bg0, bg1))
            EDGE = EDGE_S if is_s else EDGE_P
            for g in range(4):
                on_pool = False
                ett = ptt if on_pool else vtt
                cur = PP[:, g]
                TA_, TB_ = (TA2, TB2) if on_pool else (TA, TB)
                r_ta_, r_tb_ = (P.R("ta2"), P.R("tb2")) if on_pool else (r_ta, r_tb)
                ett(TA_[:, :, 1:LP], cur[:, :, 0:LP - 1], cur[:, :, 1:LP], ALU.add, [r_pp[g]], [r_ta_])
                wb, rwb = TA_, r_ta_
                if g >= 1:
                    ett(TB_[:, :, 2:LP - 1], TA_[:, :, 1:LP - 2], TA_[:, :, 3:LP], ALU.add, [r_ta_], [r_tb_])
                    wb, rwb = TB_, r_tb_
                if g >= 2:
                    ett(TA_[:, :, 4:LP - 3], TB_[:, :, 2:LP - 5], TB_[:, :, 6:LP - 1], ALU.add, [r_tb_], [r_ta_])
                    wb, rwb = TA_, r_ta_
                if g >= 3:
                    ett(TB_[:, :, 8:LP - 7], TA_[:, :, 4:LP - 11], TA_[:, :, 12:LP - 3], ALU.add, [r_ta_], [r_tb_])
                    wb, rwb = TB_, r_tb_
                for s in range(NS):
                    ett(wb[:, s, 8:16], wb[:, s, 8:16], EDGE[:, g, 0:8], ALU.mult, [rwb, r_const], [rwb])
                    ett(wb[:, s, L:L + 8], wb[:, s, L:L + 8], EDGE[:, g, 8:16], ALU.mult, [rwb, r_const], [rwb])
                wwin = (2, 4, 8, 16)[g]
                if on_pool:
                    pts(wb[:, :, 8:L + 8], wb[:, :, 8:L + 8], 1.0 / wwin, [rwb], [rwb])
                    ptt(DT[:, g, :].rearrange("p (s l) -> p s l", s=NS), wb[:, :, 8:L + 8], cur[:, :, 8:L + 8],
                        ALU.subtract, [rwb, r_pp[g]], [r_dt[g]])
                else:
                    vstt(DT[:, g, :].rearrange("p (s l) -> p s l", s=NS), wb[:, :, 8:L + 8], 1.0 / wwin,
                         cur[:, :, 8:L + 8], ALU.mult, ALU.subtract, [rwb, r_pp[g]], [r_dt[g]])
            for g in range(4):
                for ti in range(2):
                    b = poolA.next()
                    mm(bank_ap(b)[:, 0:512], WPOOL[:, g, :], DT[:, g, ti * 512:(ti + 1) * 512], True, True,
                       [r_wpool, r_dt[g]], [banks[b]], inc=True)
                    actv(YT[:, g, ti * 512:(ti + 1) * 512], bank_ap(b)[:, 0:512], AF.Identity, [banks[b], r_const],
                         [r_yt[g]], scale=PSC[:, g:g + 1])

            if LV[stops[half]] < 4:
                return
            P.barrier()
            A.reset(m3)
            MRG = A.take(8 * 1024 * 2, BF16, "p (j t) -> p j t", j=8)
            SG = A.take(4 * 512 * 4, F32, "p (s t) -> p s t", s=4)
            TM = A.take(4 * 512 * 4, F32, "p (s t) -> p s t", s=4)
            r_mrg = [P.R("mrg", j) for j in range(8)]
            r_sg = [P.R("sg", s) for s in range(4)]
            r_tm = [P.R("tm", s) for s in range(4)]
            mc = {"n": 0}
            poolM = PsumPool(P, [0, 1, 2, 3, 4, 5, 6, 7], "M")
            do_n3 = is_s and stops[1] == "merge"
            if do_n3:
                NTA = A.take(2 * 512 * 4, F32, "p (s t) -> p s t", s=2)
                NTB = A.take(2 * 512 * 4, F32, "p (s t) -> p s t", s=2)
                NRS = A.take(2 * 512 * 4, F32, "p (s t) -> p s t", s=2)
            for jp in range(4):
                if jp == 0 and pre_gates is not None:
                    u0, u1 = pre_u0, pre_u1
                else:
                    u0 = wacq(("g0", half, jp))
                    u1 = wacq(("g1", half, jp))
                u2 = wacq(("brp", half, jp))
                u3 = wacq(("bra", half, jp))
                for bb in range(2):
                    j = 2 * jp + bb
                    cs = slice(bb * 128, (bb + 1) * 128)
                    if do_n3 and j == 1:
                        for tile in TT_ALL[0:2]:
                            norm_tile(2, tile, tmp=(NTA, NTB, NRS, "m"))
                    for ti, (t0, n, _c) in enumerate(own):
                        tl = slice(ti * 512, (ti + 1) * 512)
                        pre = (j == 0 and pre_gates is not None)
                        if pre:
                            ba, bbk = poolM.next(), poolM.next()
                            bg0, bg1 = pre_gates[ti]
                        else:
                            ba, bbk, bg0, bg1 = poolM.next(), poolM.next(), poolM.next(), poolM.next()
                        for kc in range(4):
                            mm(bank_ap(ba)[:, :], u2[1][:, kc, cs], YT[:, kc, tl], kc == 0, kc == 3,
                               [u2[2], r_yt[kc]], [banks[ba]], inc=(kc == 3))
                        for kc in range(4):
                            mm(bank_ap(bbk)[:, :], u3[1][:, kc, cs], ATT[:, kc, tl], kc == 0, kc == 3,
                               [u3[2], r_att[kc]], [banks[bbk]], inc=(kc == 3))
                        for kc in range(8 if not pre else 0):
                            mm(bank_ap(bg0)[:, :], u0[1][:, kc, cs], XN[:, kc, t0:t0 + 512], kc == 0, kc == 7,
                               [u0[2], rxn(kc, t0)], [banks[bg0]], inc=(kc == 7))
                        for kc in range(8 if not pre else 0):
                            mm(bank_ap(bg1)[:, :], u1[1][:, kc, cs], XN[:, kc, t0:t0 + 512], kc == 0, kc == 7,
                               [u1[2], rxn(kc, t0)], [banks[bg1]], inc=(kc == 7))
                        s0 = (mc["n"] % 2) * 2
                        mc["n"] += 1
                        actv(SG[:, s0, :], bank_ap(bg0)[:, :], AF.Sigmoid, [banks[bg0]], [r_sg[s0]])
                        actv(SG[:, s0 + 1, :], bank_ap(bg1)[:, :], AF.Sigmoid, [banks[bg1]], [r_sg[s0 + 1]])
                        vtt(TM[:, s0, :], SG[:, s0, :], bank_ap(ba)[:, :], ALU.mult, [r_sg[s0], banks[ba]], [r_tm[s0]])
                        vtt(TM[:, s0 + 1, :], SG[:, s0 + 1, :], bank_ap(bbk)[:, :], ALU.mult,
                            [r_sg[s0 + 1], banks[bbk]], [r_tm[s0 + 1]])
                        vtt(MRG[:, j, tl], TM[:, s0, :], TM[:, s0 + 1, :], ALU.add, [r_tm[s0], r_tm[s0 + 1]], [r_mrg[j]])
                for uq in (u0, u1, u2, u3):
                    wrel(uq[0])
            for mu in range(4):
                uo = wacq(("wo", half, mu))
                for bb in range(2):
                    m = 2 * mu + bb
                    for ti, (t0, n, _c) in enumerate(own):
                        by = poolB.next()
                        for j in range(8):
                            mm(bank_ap(by)[:, :], uo[1][:, j, bb * 128:(bb + 1) * 128], MRG[:, j, ti * 512:(ti + 1) * 512],
                               j == 0, j == 7, [uo[2], r_mrg[j]], [banks[by]], inc=(j == 7))
                        vstt(X[:, m, t0:t0 + 512], bank_ap(by)[:, :], DER[:, 5, m, cond:cond + 1], X[:, m, t0:t0 + 512],
                             ALU.mult, ALU.add, [banks[by], r_der[5]], [rx(m, t0)])
                wrel(uo[0])

        ada_state = {"u": 12}

        def ff1_after_block(bi):
            us = []
            for _ in range(3):
                if ada_state["u"] < 36:
                    us.append(ada_state["u"])
                    ada_state["u"] += 1
            if us:
                ada_units(us)
            if bi == 8:
                assert ada_state["u"] == 36
                for i in range(3, 9):
                    derive(i)

        def norm2_a(ti):
            if with_mixer:
                norm_tile(1, TT_ALL[ti], part="a")

        def norm2_b(ti):
            if with_mixer:
                norm_tile(1, TT_ALL[ti], part="b")

        def ff1_blk0_hook():
            ada_units(list(range(8, 12)))
            derive(2)

        full_mixer = with_mixer and stops[0] is not None
        left = ff_phase(0, 0, TT_ALL, after_block=ff1_after_block, tile_done_a=norm2_a, tile_done_b=norm2_b,
                        lazy_load=load_x_tile, blk0_hook=ff1_blk0_hook, defer_last=full_mixer)

        if with_mixer:
            if stops[0] is not None:
                mixer_half(0, q_hooks=[(lambda ti=ti: norm2_b(ti)) for ti in left])
            if stops[1] is not None:
                mixer_half(1)
            P.barrier()
            A.reset(ffend)

        ff_phase(1, 2, TT_OWN, tile_done_b=out_tile,
                 norm_done=(0, 1) if (with_mixer and stops[1] == "merge") else ())

        final_waits = {}
        for t in out_toks:
            final_waits[t.sem] = max(final_waits.get(t.sem, 0), t.val)
        assert wstate["next_acq"] == len(wsched), (wstate["next_acq"], len(wsched))

        block = es.enter_context(nc.Block())

        @block.tensor
        def _(e):
            P.replay("pe", e)

        @block.scalar
        def _(e):
            P.replay("act", e)

        @block.vector
        def _(e):
            P.replay("dve", e)

        @block.gpsimd
        def _(e):
            P.replay("pool", e)

        @block.sync
        def _(e):
            P.replay("sp", e, final_waits=list(final_waits.items()))

    return nc


_NC_CACHE = {}


def _get_nc(with_mixer=True):
    key = with_mixer
    if key not in _NC_CACHE:
        _NC_CACHE[key] = build_program(with_mixer=with_mixer)
    return _NC_CACHE[key]


def _colT(v):
    return np.ascontiguousarray(np.asarray(v, np.float32).reshape(8, 128).T)


def make_in_maps(inp):
    f = lambda a: np.ascontiguousarray(np.asarray(a, np.float32))
    x_prompt = f(inp["x_prompt"])
    x_sample = f(inp["x_sample"])
    shared = {
        "w_ada": f(inp["w_ada"][0]),
        "b_adaT": np.ascontiguousarray(f(inp["b_ada"][0]).reshape(72, 128).T),
        "gnT": np.ascontiguousarray(np.concatenate(
            [_colT(inp["g_ff1"][0]), _colT(inp["g_mix"][0]), _colT(inp["g_ff2"][0])], axis=1)),
        "w_ff1_in": f(inp["w_ff1_in"][0]), "w_ff1_out": f(inp["w_ff1_out"][0]),
        "w_ff2_in": f(inp["w_ff2_in"][0]), "w_ff2_out": f(inp["w_ff2_out"][0]),
        "ident": np.eye(128, dtype=np.float32),
        "w_in": f(inp["w_in"][0]), "w_pool": f(inp["w_pool"][0]),
        "w_br_pool": f(inp["w_br_pool"][0]), "w_br_att": f(inp["w_br_att"][0]), "w_out": f(inp["w_out"][0]),
    }
    gainT = np.empty((128, 6), np.float32)
    gainT[:, 0] = np.tile(f(inp["q_gain"][0]), 2)
    gainT[:, 1] = np.tile(f(inp["k_gain"][0]), 2)
    gainT[:, 2:6] = f(inp["pool_scale"][0]).reshape(4, 128).T
    shared["gainT"] = gainT
    rpb = f(inp["rpb"][0])
    ii = np.arange(2)[:, None, None, None]
    ck = np.arange(64)[None, :, None, None]
    ss = np.arange(14)[None, None, :, None]
    cq = np.arange(64)[None, None, None, :]
    dr = np.broadcast_to(ss + ii, (2, 64, 14, 64))
    dc = np.broadcast_to(np.clip(ck - cq + 15, 0, 30), (2, 64, 14, 64))
    eb = rpb[:, dr, dc]
    shared["eb2src"] = np.ascontiguousarray(eb.transpose(1, 2, 0, 3, 4).reshape(128, 8 * 14 * 64))
    colq = np.arange(64)
    c0 = np.clip(colq - 8, 0, 48)
    ok = (colq[:, None] >= c0[None, :]) & (colq[:, None] < c0[None, :] + 16)
    shared["colmask"] = np.ascontiguousarray(np.tile(ok.astype(np.float32), (2, 1)))
    tiles = qb_tiles()

    def edge_tab(L, start_real, end_real):
        e = np.ones((4, 16), np.float32)
        for g, w in enumerate((2, 4, 8, 16)):
            for t in range(8):
                if start_real and t < w // 2:
                    e[g, t] = w / float(t + w // 2)
                te = L - 8 + t
                if end_real and te + w // 2 > L:
                    e[g, 8 + t] = w / float(L - te + w // 2)
        return e.reshape(64)
    maps = []
    for i in range(NCORES):
        b, par = i // 2, i % 2
        xp = x_prompt[4 * i:4 * i + 4].reshape(NTOK_P, D)
        own = x_sample[b, par * 1024:(par + 1) * 1024]
        halo = x_sample[b, 1024:1280] if par == 0 else x_sample[b, 768:1024]
        xin = np.ascontiguousarray(np.concatenate([xp, own, halo], axis=0))
        condT = np.empty((128, 8, 2), np.float32)
        condT[:, :, 0] = _colT(inp["c_ctx"])
        condT[:, :, 1] = _colT(inp["c"][b])
        cst = np.zeros((128, NCST), np.float32)
        cst[:, 0] = 1.0 if par == 1 else 0.0
        cst[:, 1] = 1.0 if par == 0 else 0.0
        cst[:, 2:66] = edge_tab(256, True, True)[None, :]
        cst[:, 66:130] = edge_tab(1024, par == 0, par == 1)[None, :]
        t = 0
        for bq in range(8):
            for (kind, idx, delta) in tiles[bq]:
                for i2 in range(2):
                    for jq in range(2):
                        rq = par * 16 + 2 * bq + jq
                        r0 = min(max(rq - 4, 0), 24)
                        if kind == "o":
                            rk = par * 16 + 2 * idx + i2
                        else:
                            rk = (16 if par == 0 else 12) + 2 * idx + i2
                        valid = (r0 <= rk <= r0 + 7) and (rk - rq == 2 * delta + i2 - jq)
                        cst[i2 * 64:(i2 + 1) * 64, 130 + 2 * t + jq] = 1.0 if valid else 0.0
                t += 1
        assert t == NVM
        m = dict(shared)
        m["cst"] = cst
        m["ck"] = np.ascontiguousarray(f(inp["cache_k"][b, 0]).reshape(512, 512))
        m["cv"] = np.ascontiguousarray(f(inp["cache_v"][b, 0]).reshape(512, 512))
        m["xin"] = xin
        m["condT"] = np.ascontiguousarray(condT.reshape(128, 16))
        maps.append(m)
    return maps


def kernel(**inputs):
    nc = _get_nc(True)
    maps = make_in_maps(inputs)
    res = run_bass_kernel_spmd(nc, maps, core_ids=list(range(NCORES)))
    B, S = 32, 256
    y_prompt = np.empty((B, S, D), np.float32)
    y_sample = np.empty((4, 2048, D), np.float32)
    new_k = np.empty((B, 1, S, 8, 64), np.float32)
    new_v = np.empty((B, 1, S, 8, 64), np.float32)
    for i in range(NCORES):
        r = res.results[i]
        b, par = i // 2, i % 2
        yo = r["yo"]
        y_prompt[4 * i:4 * i + 4] = yo[:NTOK_P].reshape(4, S, D)
        y_sample[b, par * 1024:(par + 1) * 1024] = yo[NTOK_P:]
        new_k[4 * i:4 * i + 4, 0] = r["nk"].reshape(4, S, 8, 64)
        new_v[4 * i:4 * i + 4, 0] = r["nv"].reshape(4, S, 8, 64)
    return (y_prompt, y_sample, new_k, new_v)
```
